# Optimizing a Trainium2 kernel written in Bass

```python
import jax, jax.numpy as jnp
from jax import lax
import numpy as np

D_MODEL = 1024
BATCH = 16
SEQ = 4096
DEPTH = 4

HEAD_DIM = 64
GRID_W = 64
D_FOURIER = D_MODEL // 4
D_LRU = 3 * D_MODEL // 8
D_NA = D_MODEL - D_FOURIER - D_LRU
D_MIX = D_FOURIER + D_LRU + D_NA
N_FOURIER_GROUPS = D_FOURIER // HEAD_DIM
N_LRU_HEADS = D_LRU // HEAD_DIM
N_NA_HEADS = D_NA // HEAD_DIM
D_IN = D_FOURIER + 2 * D_LRU + 3 * D_NA
CONV_W = 4
CONV_PAD_LEFT = 2
CONV_PAD_RIGHT = CONV_W - 1 - CONV_PAD_LEFT
LRU_C = 8.0
NA_KH = 8
NA_KW = 16
D_FF = -(-(8 * D_MODEL) // (3 * 256)) * 256
N_MOD = 6
EPS = 1e-6

kernel_name = "hybrid_fnet_rglru_natten_encoder"


def rms_norm(x, g):
    xf = x.astype(jnp.float32)
    y = xf * lax.rsqrt(jnp.mean(xf * xf, axis=-1, keepdims=True) + EPS)
    return (y * g.astype(jnp.float32)).astype(x.dtype)


def modulate(h, shift, scale):
    return h * (1 + scale[:, None, :]) + shift[:, None, :]


def fourier_mix(u, w_map):
    B, S, _ = u.shape
    ug = u.astype(jnp.float32).reshape(B, S, N_FOURIER_GROUPS, HEAD_DIM)
    f = jnp.real(jnp.fft.fft2(ug, axes=(1, 3), norm="ortho")).astype(u.dtype)
    y = jnp.einsum('bsgd,gde->bsge', f, w_map)
    return y.reshape(B, S, D_FOURIER)


def centred_depthwise_conv(u, w, b):
    S = u.shape[1]
    up = jnp.pad(u, ((0, 0), (CONV_PAD_LEFT, CONV_PAD_RIGHT), (0, 0)))
    y = sum(w[k] * up[:, k:k + S, :] for k in range(CONV_W))
    return y + b


def _linear_recurrence_combine(left, right):
    a_l, b_l = left
    a_r, b_r = right
    return a_l * a_r, a_r * b_l + b_r


def rg_lru_direction(u, w_a, b_a, w_x, b_x, lam, reverse):
    B, S, _ = u.shape
    uh = u.reshape(B, S, N_LRU_HEADS, HEAD_DIM)
    ra = jnp.einsum('bshd,hde->bshe', uh, w_a).reshape(B, S, D_LRU) + b_a
    rx = jnp.einsum('bshd,hde->bshe', uh, w_x).reshape(B, S, D_LRU) + b_x
    r = jax.nn.sigmoid(ra.astype(jnp.float32))
    i = jax.nn.sigmoid(rx.astype(jnp.float32))
    log_a = -LRU_C * r * jax.nn.softplus(-lam.astype(jnp.float32))
    a = jnp.exp(log_a)
    bterm = jnp.sqrt(-jnp.expm1(2.0 * log_a)) * (i * u.astype(jnp.float32))
    _, h = lax.associative_scan(_linear_recurrence_combine, (a, bterm), axis=1, reverse=reverse)
    return h


def neighbourhood_attention(q, k, v, rpb):
    B, S, H, dh = q.shape
    rows = S // GRID_W
    kh = min(NA_KH, rows)
    kw = NA_KW
    qg = q.reshape(B, rows, GRID_W, H, dh) * (HEAD_DIM ** -0.5)
    kg = k.reshape(B, rows, GRID_W, H, dh)
    vg = v.reshape(B, rows, GRID_W, H, dh)
    row_start = jnp.clip(jnp.arange(rows) - kh // 2, 0, rows - kh)
    cols = jnp.arange(GRID_W)
    col_start = jnp.clip(cols - kw // 2, 0, GRID_W - kw)
    col_idx = col_start[:, None] + jnp.arange(kw)[None, :]
    dc = col_idx - cols[:, None] + (NA_KW - 1)
    bias_c = rpb[:, :, dc]

    def one_row(r):
        rs = row_start[r]
        k_band = lax.dynamic_slice_in_dim(kg, rs, kh, axis=1)
        v_band = lax.dynamic_slice_in_dim(vg, rs, kh, axis=1)
        k_win = k_band[:, :, col_idx]
        v_win = v_band[:, :, col_idx]
        q_row = lax.dynamic_index_in_dim(qg, r, axis=1, keepdims=False)
        s = jnp.einsum('bqhd,brqkhd->bhqrk', q_row, k_win)
        dr = rs + jnp.arange(kh) - r + (NA_KH - 1)
        bias = jnp.transpose(bias_c[:, dr], (0, 2, 1, 3))
        s = s.astype(jnp.float32) + bias.astype(jnp.float32)
        p = jax.nn.softmax(s.reshape(B, H, GRID_W, kh * kw), axis=-1)
        p = p.reshape(B, H, GRID_W, kh, kw).astype(v.dtype)
        return jnp.einsum('bhqrk,brqkhd->bqhd', p, v_win)

    out = lax.map(one_row, jnp.arange(rows, dtype=jnp.int32))
    return jnp.transpose(out, (1, 0, 2, 3, 4)).reshape(B, S, H * dh)


def setup_inputs(seed: int = 0) -> dict:
    key = jax.random.key(seed)
    ks = jax.random.split(key, 24)
    f32 = jnp.float32

    def nrm(k, shape, scale):
        return jax.random.normal(k, shape, f32) * scale

    u = jax.random.uniform(ks[12], (DEPTH, 2, D_LRU), f32, minval=0.9, maxval=0.999)
    s = u ** (1.0 / LRU_C)
    lam = jnp.log(s) - jnp.log1p(-s)
    return {
        "x": nrm(ks[0], (BATCH, SEQ, D_MODEL), 1.0),
        "c": nrm(ks[1], (BATCH, D_MODEL), 1.0),
        "w_ada": nrm(ks[2], (DEPTH, D_MODEL, N_MOD * D_MODEL), 0.5 * D_MODEL ** -0.5),
        "b_ada": nrm(ks[3], (DEPTH, N_MOD * D_MODEL), 0.02),
        "g_mix": 1.0 + nrm(ks[4], (DEPTH, D_MODEL), 0.02),
        "g_ffn": 1.0 + nrm(ks[5], (DEPTH, D_MODEL), 0.02),
        "w_in": nrm(ks[6], (DEPTH, D_MODEL, D_IN), D_MODEL ** -0.5),
        "w_fourier": nrm(ks[7], (DEPTH, N_FOURIER_GROUPS, HEAD_DIM, HEAD_DIM), HEAD_DIM ** -0.5),
        "conv_w": nrm(ks[8], (DEPTH, CONV_W, D_LRU), 0.5),
        "conv_b": nrm(ks[9], (DEPTH, D_LRU), 0.02),
        "lru_w_a": nrm(ks[10], (DEPTH, 2, N_LRU_HEADS, HEAD_DIM, HEAD_DIM), HEAD_DIM ** -0.5),
        "lru_b_a": nrm(ks[11], (DEPTH, 2, D_LRU), 0.02),
        "lru_w_x": nrm(ks[13], (DEPTH, 2, N_LRU_HEADS, HEAD_DIM, HEAD_DIM), HEAD_DIM ** -0.5),
        "lru_b_x": nrm(ks[14], (DEPTH, 2, D_LRU), 0.02),
        "lru_lambda": lam,
        "na_rpb": nrm(ks[15], (DEPTH, N_NA_HEADS, 2 * NA_KH - 1, 2 * NA_KW - 1), 0.1),
        "w_out": nrm(ks[16], (DEPTH, D_MIX, D_MODEL), D_MIX ** -0.5),
        "w_ffn_gate": nrm(ks[17], (DEPTH, D_MODEL, D_FF), D_MODEL ** -0.5),
        "w_ffn_up": nrm(ks[18], (DEPTH, D_MODEL, D_FF), D_MODEL ** -0.5),
        "w_ffn_down": nrm(ks[19], (DEPTH, D_FF, D_MODEL), D_FF ** -0.5),
        "g_final": 1.0 + nrm(ks[20], (D_MODEL,), 0.02),
    }


def reference(x, c, w_ada, b_ada, g_mix, g_ffn, w_in, w_fourier, conv_w, conv_b,
              lru_w_a, lru_b_a, lru_w_x, lru_b_x, lru_lambda, na_rpb, w_out,
              w_ffn_gate, w_ffn_up, w_ffn_down, g_final):
    B, S, _ = x.shape
    c_act = jax.nn.silu(c)
    o1 = D_FOURIER
    o2 = o1 + D_LRU
    o3 = o2 + D_LRU
    o4 = o3 + D_NA
    o5 = o4 + D_NA
    for l in range(DEPTH):
        mod = c_act @ w_ada[l] + b_ada[l]
        sh1, sc1, gt1, sh2, sc2, gt2 = jnp.split(mod, N_MOD, axis=-1)

        h = modulate(rms_norm(x, g_mix[l]), sh1, sc1)
        p = h @ w_in[l]
        p_f, p_x, p_g = p[..., :o1], p[..., o1:o2], p[..., o2:o3]
        q = p[..., o3:o4].reshape(B, S, N_NA_HEADS, HEAD_DIM)
        k = p[..., o4:o5].reshape(B, S, N_NA_HEADS, HEAD_DIM)
        v = p[..., o5:].reshape(B, S, N_NA_HEADS, HEAD_DIM)

        y_f = fourier_mix(p_f, w_fourier[l])

        u = centred_depthwise_conv(p_x, conv_w[l], conv_b[l])
        h_fwd = rg_lru_direction(u, lru_w_a[l, 0], lru_b_a[l, 0], lru_w_x[l, 0],
                                 lru_b_x[l, 0], lru_lambda[l, 0], reverse=False)
        h_bwd = rg_lru_direction(u, lru_w_a[l, 1], lru_b_a[l, 1], lru_w_x[l, 1],
                                 lru_b_x[l, 1], lru_lambda[l, 1], reverse=True)
        y_l = jax.nn.gelu(p_g) * (h_fwd + h_bwd).astype(p_g.dtype)

        y_n = neighbourhood_attention(q, k, v, na_rpb[l])

        y = jnp.concatenate([y_f, y_l, y_n], axis=-1) @ w_out[l]
        x = x + gt1[:, None, :] * y

        h = modulate(rms_norm(x, g_ffn[l]), sh2, sc2)
        f = (jax.nn.silu(h @ w_ffn_gate[l]) * (h @ w_ffn_up[l])) @ w_ffn_down[l]
        x = x + gt2[:, None, :] * f
    return rms_norm(x, g_final)
```

```python
import math
from contextlib import ExitStack
import numpy as np
import concourse.bass as bass
import concourse.mybir as mybir
from concourse.bass_utils import run_bass_kernel_spmd

F32 = mybir.dt.float32
BF16 = mybir.dt.bfloat16
AF = mybir.ActivationFunctionType
ALU = mybir.AluOpType

EPOCH = 30000
SAME_ENGINE_SYNC = True

D = 1024
S = 4096
DEPTH = 4
D_IN = 2176
D_FF = 2816
NFF = 22
TT = 512
NTT = 8
EPS = 1e-6
N_CORES = 8


class Buf:
    def __init__(self, prog, name, n=1):
        self.name = name
        self.n = n
        self.lw = [None] * n
        self.rd = [[] for _ in range(n)]
        prog.bufs.append(self)

    def __getitem__(self, idx):
        if isinstance(idx, slice):
            return [(self, i) for i in range(*idx.indices(self.n))]
        if isinstance(idx, (list, tuple)):
            return [(self, i) for i in idx]
        return [(self, idx)]

    def all(self):
        return [(self, i) for i in range(self.n)]

    def reset(self):
        self.lw = [None] * self.n
        self.rd = [[] for _ in range(self.n)]


def _flat(lst):
    out = []
    for x in lst:
        if isinstance(x, Buf):
            out.extend(x.all())
        elif isinstance(x, list):
            out.extend(_flat(x))
        else:
            out.append(x)
    return out


class Op:
    __slots__ = ("eng", "fn", "chan", "seq", "waits", "signal", "tick", "kdone",
                 "is_dma", "ndma", "idx")


class Prog:
    ENGS = ("pe", "act", "dve", "pool", "sp")

    def __init__(self, nc, n_dma_sems=6):
        self.nc = nc
        self.n_dma_sems = n_dma_sems
        self.bufs = []
        self.tick = {e: 0 for e in self.ENGS}
        self.dtick = {}
        self.sems = {}
        self.nops = 0
        self.total = {e: 0 for e in self.ENGS}
        self.nflush = 0
        self.stop = 10 ** 9
        self.maxops = 10 ** 9
        self.verbose = False
        self._reset()

    def _reset(self):
        self.ops = {e: [] for e in self.ENGS}
        self.know = {e: {} for e in self.ENGS}
        self.chan_seq = {}
        self.chan_last = {}
        self.dma_rr = {e: 0 for e in self.ENGS}
        for b in self.bufs:
            b.reset()

    def buf(self, name, n=1):
        return Buf(self, name, n)

    def op(self, eng, fn, reads=(), writes=(), dma=0):
        if self.nflush >= self.stop or self.nops >= self.maxops:
            return None
        X = Op()
        X.eng = eng
        X.fn = fn
        X.is_dma = dma > 0
        X.ndma = dma
        X.signal = X.is_dma
        X.tick = None
        X.idx = self.nops
        self.nops += 1
        reads = _flat(reads)
        writes = _flat(writes)
        if X.is_dma:
            k = self.dma_rr[eng]
            self.dma_rr[eng] = (k + 1) % self.n_dma_sems
            X.chan = ("dma", eng, k)
        else:
            X.chan = eng
        X.seq = self.chan_seq.get(X.chan, 0) + 1
        self.chan_seq[X.chan] = X.seq
        deps = {}
        for (b, i) in reads:
            w = b.lw[i]
            if w is not None:
                deps[w.idx] = (w, True)
        for (b, i) in writes:
            w = b.lw[i]
            if w is not None:
                deps[w.idx] = (w, True)
            for r in b.rd[i]:
                if r.idx not in deps:
                    deps[r.idx] = (r, False)
        if X.is_dma:
            p = self.chan_last.get(X.chan)
            if p is not None:
                deps[p.idx] = (p, True)
            self.chan_last[X.chan] = X
        K = self.know[eng]
        waits = []
        for di in sorted(deps.keys(), reverse=True):
            d, hard = deps[di]
            if (not d.is_dma) and (not X.is_dma) and d.eng == eng:
                if eng == "pe" or not hard or not SAME_ENGINE_SYNC:
                    continue
            if K.get(d.chan, 0) >= d.seq:
                continue
            d.signal = True
            waits.append(d)
            for c, s in d.kdone.items():
                if K.get(c, 0) < s:
                    K[c] = s
        X.waits = waits
        kd = dict(K)
        kd[X.chan] = X.seq
        X.kdone = kd
        for (b, i) in reads:
            b.rd[i].append(X)
        for (b, i) in writes:
            b.lw[i] = X
            b.rd[i] = []
        self.ops[eng].append(X)
        return X

    def _sem(self, key):
        s = self.sems.get(key)
        if s is None:
            s = self.nc.alloc_semaphore("s_" + "_".join(str(k) for k in key))
            self.sems[key] = s
        return s

    def flush(self):
        self.nflush += 1
        if self.nflush > self.stop:
            return
        if self.verbose:
            print("flush", self.nflush, "nops", self.nops, {e: len(v) for e, v in self.ops.items()}, flush=True)
        last_dmas = list(self.chan_last.values())
        if last_dmas:
            fb = Buf(self, "_fence")
            for X in last_dmas:
                fb.rd[0].append(X)
            self.op("sp", None, writes=[fb])
            self.bufs.remove(fb)
        for e in self.ENGS:
            for X in self.ops[e]:
                if X.is_dma:
                    c = self.dtick.get(X.chan, 0) + X.ndma
                    self.dtick[X.chan] = c
                    X.tick = c * 16
                elif X.signal:
                    self.tick[e] += 1
                    X.tick = self.tick[e]
            self.total[e] += len(self.ops[e])

        def sem_val(d):
            if d.is_dma:
                return self._sem(d.chan), d.tick
            ep = (d.tick - 1) // EPOCH
            return self._sem((d.eng, ep)), d.tick - ep * EPOCH

        def run(engobj, lst):
            for X in lst:
                for d in reversed(X.waits):
                    s, v = sem_val(d)
                    engobj.wait_ge(s, v)
                if X.fn is None:
                    continue
                if X.is_dma:
                    s = self._sem(X.chan)
                    instrs = X.fn(engobj)
                    if not isinstance(instrs, (list, tuple)):
                        instrs = [instrs]
                    assert len(instrs) == X.ndma, (len(instrs), X.ndma)
                    for ins in instrs:
                        ins.then_inc(s, 16)
                else:
                    ins = X.fn(engobj)
                    if X.signal:
                        s, _ = sem_val(X)
                        ins.then_inc(s, 1)

        ops = self.ops
        with self.nc.Block() as block:
            @block.tensor
            def _(eng):
                run(eng, ops["pe"])

            @block.scalar
            def _(eng):
                run(eng, ops["act"])

            @block.vector
            def _(eng):
                run(eng, ops["dve"])

            @block.gpsimd
            def _(eng):
                run(eng, ops["pool"])

            @block.sync
            def _(eng):
                run(eng, ops["sp"])
        self._reset()


def _consts():
    d = np.arange(64)
    ang = 2.0 * np.pi * np.outer(d, d) / 64.0
    C64 = np.cos(ang)
    S64 = np.sin(ang)
    cd = np.stack([C64 / 512.0, -S64 / 512.0], axis=1).astype(np.float32)
    F = np.zeros((64, 2, 2, 64), np.float64)
    F[:, 0, 0, :] = C64
    F[:, 0, 1, :] = -S64
    F[:, 1, 0, :] = S64
    F[:, 1, 1, :] = C64
    F = F.reshape(64, 2, 128).astype(np.float32)
    s2 = np.arange(64)[:, None, None]
    k1 = np.arange(64)[None, :, None]
    k2 = np.arange(64)[None, None, :]
    th = 2.0 * np.pi * ((s2 * (k1 + 64 * k2)) % 4096) / 4096.0
    Mc = np.stack([np.cos(th), np.sin(th)], axis=2).astype(np.float32)
    qc = np.arange(64)
    cs = np.clip(qc - 8, 0, 48)
    kc = np.arange(64)[:, None]
    cm = ((kc >= cs[None, :]) & (kc < cs[None, :] + 16)).astype(np.float32)
    cm = np.concatenate([cm, cm], axis=0)
    return cd, F, Mc.reshape(64, 64 * 2 * 64), cm


def build(NSEQ=2, NL=DEPTH, stop=None, maxops=None):
    nc = bass.Bass("TRN2", target_bir_lowering=False)
    P = Prog(nc)
    if stop is not None:
        P.stop = stop
    if maxops is not None:
        P.maxops = maxops

    def dram(name, shape, dt, kind="ExternalInput"):
        return nc.dram_tensor(name, list(shape), dt, kind=kind)

    xT_h = dram("xT", [NSEQ, 8, 128, S], F32)
    cT_h = dram("cT", [128, 8, 2], F32)
    w_ada_h = dram("w_ada", [NL, D, 6 * D], F32)
    b_adaT_h = dram("b_adaT", [NL, 128, 48], F32)
    gT_h = dram("gT", [128, NL, 2, 8], F32)
    gfin_h = dram("gfin", [128, 8], F32)
    w_in_h = dram("w_in", [NL, D, D_IN], F32)
    w_out_h = dram("w_out", [NL, D, D], F32)
    w_gate_h = dram("w_gate", [NL, D, D_FF], F32)
    w_up_h = dram("w_up", [NL, D, D_FF], F32)
    w_down_h = dram("w_down", [NL, D_FF, D], F32)
    w_four_h = dram("w_four", [NL, 4, 64, 64], F32)
    convw_h = dram("convw", [128, NL, 3, 4], F32)
    convb_h = dram("convb", [128, NL, 3], F32)
    lru_wa_h = dram("lru_wa", [NL, 2, 6, 64, 64], F32)
    lru_wx_h = dram("lru_wx", [NL, 2, 6, 64, 64], F32)
    lruv_h = dram("lruv", [128, 3, NL, 2, 3], F32)
    rpb_h = dram("rpbtab", [NL, 90, 64, 64], F32)
    cd_h = dram("c_cd", [64, 2, 64], F32)
    F_h = dram("c_F", [64, 2, 128], F32)
    Mc_h = dram("c_Mc", [64, 8192], F32)
    cm_h = dram("c_cm", [128, 64], F32)
    out_h = dram("outT", [NSEQ, 8, 128, S], F32, kind="ExternalOutput")
    xs_h = dram("xs", [NSEQ, 8, 128, S], F32, kind="Internal")
    pU_h = dram("pU", [NSEQ, 2, 128, S], BF16, kind="Internal")
    pgl_h = dram("pgl", [NSEQ, 3, 128, S], BF16, kind="Internal")
    pq_h = dram("pq", [NSEQ, 3, 128, S], BF16, kind="Internal")
    pk_h = dram("pk", [NSEQ, 3, 128, S], BF16, kind="Internal")
    ppx_h = dram("ppx", [NSEQ, 3, 128, S], F32, kind="Internal")
    pv_h = dram("pv", [NSEQ, 32, 128, 384], BF16, kind="Internal")
    yT_h = dram("yTd", [NSEQ, 8, 128, S], BF16, kind="Internal")
    h2_h = dram("h2d", [NSEQ, 8, 128, S], BF16, kind="Internal")

    xT, w_ada, w_in, w_out = xT_h.ap(), w_ada_h.ap(), w_in_h.ap(), w_out_h.ap()
    w_gate, w_up, w_down = w_gate_h.ap(), w_up_h.ap(), w_down_h.ap()
    xs, pU, pgl, pq, pk, ppx, pv, yTd = (h.ap() for h in (xs_h, pU_h, pgl_h, pq_h, pk_h, ppx_h, pv_h, yT_h))
    outT = out_h.ap()
    h2d = h2_h.ap()

    B_xs = P.buf("xs", NSEQ * NTT)
    B_p = P.buf("p", NSEQ)
    B_y = P.buf("yd", NSEQ)
    B_h2d = P.buf("h2d", NSEQ * NTT)
    B_out = P.buf("out")

    def I(eng, meth, reads, writes, *a, **kw):
        P.op(eng, lambda e: getattr(e, meth)(*a, **kw), reads, writes)

    def DMA(eng, out, in_, reads, writes, **kw):
        P.op(eng, lambda e: e.dma_start(out=out, in_=in_, **kw), reads, writes, dma=1)

    def DMAs(eng, pairs, reads, writes):
        P.op(eng, lambda e: [e.dma_start(out=o, in_=i) for (o, i) in pairs], reads, writes, dma=len(pairs))

    psall = nc.alloc_psum_tensor("psall", [128, 4096], F32)
    psb = [psall[:, i * 512:(i + 1) * 512] for i in range(8)]
    B_ps = P.buf("ps", 8)

    es_glob = ExitStack()

    _cnt = [0]

    def sb(es, name, shape, dt):
        _cnt[0] += 1
        return es.enter_context(nc.sbuf_tensor(f"sb{_cnt[0]}_{name}", list(shape), dt))

    ones = sb(es_glob, "ones", [128, 128], BF16)
    onesA = sb(es_glob, "onesA", [128, 128], BF16)
    onesB = sb(es_glob, "onesB", [128, 128], BF16)
    cact = sb(es_glob, "cact", [128, 8, 2], F32)
    gT = sb(es_glob, "gT", [128, NL, 2, 8], F32)
    gfin = sb(es_glob, "gfin", [128, 8], F32)
    convw = sb(es_glob, "convw", [128, NL, 3, 4], F32)
    convb = sb(es_glob, "convb", [128, NL, 3], F32)
    lruv = sb(es_glob, "lruv", [128, 3, NL * 6], F32)
    clam = sb(es_glob, "clam", [128, NL * 6], F32)
    mod = sb(es_glob, "mod", [128, 48, 2], F32)
    A1 = sb(es_glob, "A1", [128, 8, 2], F32)
    A2 = sb(es_glob, "A2", [128, 8, 2], F32)
    epsb = sb(es_glob, "epsb", [128, 1], F32)
    B_const = P.buf("const")
    B_mod = P.buf("mod")

    with ExitStack() as es:
        t = [sb(es, f"sp_t{i}", [128, NL * 6], F32) for i in range(6)]
        I("pool", "memset", [], [B_const], ones[:], 1.0)
        I("pool", "memset", [], [B_const], onesA[:], 0.0)
        I("pool", "memset", [], [B_const], onesB[:], 0.0)
        I("pool", "memset", [B_const], [B_const], onesA[:, 0:64], 1.0)
        I("pool", "memset", [B_const], [B_const], onesB[:, 64:128], 1.0)
        I("pool", "memset", [], [B_const], epsb[:], EPS)
        Bs = P.buf("setup_ld")
        DMAs("sp", [(cact[:], cT_h.ap()), (gT[:], gT_h.ap()), (gfin[:], gfin_h.ap()),
                    (convw[:], convw_h.ap()), (convb[:], convb_h.ap()),
                    (lruv[:], lruv_h.ap().rearrange("p a l d c -> p a (l d c)"))], [], [Bs])
        Bc = P.buf("setup_c")
        I("act", "activation", [Bs], [Bc], out=cact[:], in_=cact[:], func=AF.Silu)
        lam = lruv[:, 2, :]
        Bt = P.buf("setup_t")
        V = lambda *a, **k: I("dve", *a, **k)
        rw = ([Bs, Bt], [Bt])
        V("tensor_scalar", *rw, out=t[5][:], in0=lam, scalar1=-1.0, scalar2=None, op0=ALU.mult)
        V("tensor_tensor", *rw, out=t[0][:], in0=lam, in1=t[5][:], op=ALU.max)
        I("act", "activation", [Bt], [Bt], out=t[1][:], in_=t[0][:], func=AF.Exp, scale=-1.0)
        V("tensor_scalar", *rw, out=t[2][:], in0=t[1][:], scalar1=2.0, scalar2=None, op0=ALU.add)
        V("reciprocal", *rw, out=t[2][:], in_=t[2][:])
        V("tensor_tensor", *rw, out=t[2][:], in0=t[1][:], in1=t[2][:], op=ALU.mult)
        V("tensor_tensor", *rw, out=t[3][:], in0=t[2][:], in1=t[2][:], op=ALU.mult)
        V("tensor_scalar", *rw, out=t[4][:], in0=t[3][:], scalar1=1.0 / 13, scalar2=1.0 / 11, op0=ALU.mult, op1=ALU.add)
        for cf in (1.0 / 9, 1.0 / 7, 1.0 / 5, 1.0 / 3, 1.0):
            V("tensor_tensor", *rw, out=t[4][:], in0=t[4][:], in1=t[3][:], op=ALU.mult)
            V("tensor_scalar", *rw, out=t[4][:], in0=t[4][:], scalar1=cf, scalar2=None, op0=ALU.add)
        V("tensor_tensor", *rw, out=t[4][:], in0=t[4][:], in1=t[2][:], op=ALU.mult)
        V("tensor_scalar", *rw, out=t[5][:], in0=lam, scalar1=-1.0, scalar2=0.0, op0=ALU.mult, op1=ALU.max)
        V("scalar_tensor_tensor", *rw, out=t[5][:], in0=t[4][:], scalar=2.0, in1=t[5][:], op0=ALU.mult, op1=ALU.add)
        V("tensor_scalar", [Bt], [B_const], out=clam[:], in0=t[5][:], scalar1=-8.0, scalar2=None, op0=ALU.mult)
        P.flush()

    def norm_tile(xt, Bx, sq, Bsq, rs, Brs, ssb):
        I("act", "activation", [Bx], [Bsq], out=sq[:], in_=xt[:], func=AF.Square)
        for c in range(8):
            I("pe", "matmul", [Bsq, B_const], B_ps[ssb], psb[ssb][:], ones[:], sq[:, c, :],
              start=(c == 0), stop=(c == 7))
        I("act", "activation", B_ps[ssb] + [B_const], [Brs], out=rs[:], in_=psb[ssb][:], func=AF.Ln,
          scale=1.0 / D, bias=epsb[:])
        I("act", "activation", [Brs], [Brs], out=rs[:], in_=rs[:], func=AF.Exp, scale=-0.5)

    for l in range(NL):
        es_layer = ExitStack()
        w_in_sb = sb(es_layer, f"w_in_sb{l}", [128, 8, D_IN], BF16)
        EI = sb(es_layer, f"EI{l}", [128, 6, 5, 2, 64], BF16)
        EB = sb(es_layer, f"EB{l}", [128, 4, 6, 4, 2, 64], BF16)
        WG = sb(es_layer, f"WG{l}", [128, 2, 2, 3, 128], BF16)
        ABall = sb(es_layer, f"ABall{l}", [128, 2, 2, 64], BF16)
        B_win = P.buf(f"win{l}", 8)
        B_tab = P.buf(f"tab{l}", 96)
        B_wg = P.buf(f"wg{l}", 8)
        B_ab = P.buf(f"ab{l}")

        with ExitStack() as es:
            stg = [sb(es, f"ada_stg{i}", [128, 6 * D], BF16) for i in range(2)]
            cact_bf = sb(es, "cact_bf", [128, 8, 2], BF16)
            B_cbf = P.buf("cact_bf")
            I("dve", "tensor_copy", [Bc], [B_cbf], out=cact_bf[:], in_=cact[:])
            B_stg = P.buf("ada_stg", 2)
            badaT = sb(es, "badaT", [128, 48], F32)
            B_bada = P.buf("bada")
            DMA("sp", badaT[:], b_adaT_h.ap()[l], [], [B_bada])
            for k in range(8):
                DMAs("pool", [(w_in_sb[:, k, 0:1088], w_in[l, k * 128:(k + 1) * 128, 0:1088]),
                              (w_in_sb[:, k, 1088:2176], w_in[l, k * 128:(k + 1) * 128, 1088:2176])],
                     [], B_win[k])
            for k in range(8):
                DMAs("pool", [(stg[k % 2][:, h * 1536:(h + 1) * 1536], w_ada[l, k * 128:(k + 1) * 128, h * 1536:(h + 1) * 1536])
                              for h in range(4)], [], B_stg[k % 2])
                for j in range(48):
                    I("pe", "matmul", B_stg[k % 2] + [B_cbf], B_ps[0], psb[0][:, 2 * j:2 * j + 2],
                      stg[k % 2][:, j * 128:(j + 1) * 128], cact_bf[:, k, :],
                      start=(k == 0 and j == 0), stop=(k == 7 and j == 47), skip_group_check=True)
            for s in range(2):
                I("dve", "tensor_tensor", B_ps[0] + [B_bada], [B_mod], out=mod[:, :, s],
                  in0=psb[0][:, 0:96].rearrange("p (j s) -> p j s", s=2)[:, :, s], in1=badaT[:], op=ALU.add)
                I("dve", "scalar_tensor_tensor", [B_mod, Bs], [B_mod], out=A1[:, :, s], in0=mod[:, 8:16, s],
                  scalar=1.0, in1=gT[:, l, 0, :], op0=ALU.add, op1=ALU.mult)
                I("dve", "scalar_tensor_tensor", [B_mod, Bs], [B_mod], out=A2[:, :, s], in0=mod[:, 32:40, s],
                  scalar=1.0, in1=gT[:, l, 1, :], op0=ALU.add, op1=ALU.mult)
            I("pool", "memset", [], [B_wg], WG[:], 0.0)
            for dr_ in range(2):
                for ax, wh in enumerate((lru_wa_h, lru_wx_h)):
                    for hh in range(2):
                        src = wh.ap()[l, dr_, hh::2, :, :].rearrange("h d e -> d h e")
                        DMA("pool", WG[hh * 64:(hh + 1) * 64, dr_, ax, :, hh * 64:(hh + 1) * 64], src, [], B_wg[(dr_ * 2 + ax) * 2 + hh])
            cd_sb = sb(es, "cd_sb", [64, 2, 64], F32)
            wf_sb = sb(es, "wf_sb", [64, 4, 64], F32)
            B_f = P.buf("four_ld")
            DMAs("sp", [(cd_sb[:], cd_h.ap()), (wf_sb[:], w_four_h.ap()[l].rearrange("g d e -> d g e"))], [], [B_f])
            for g in range(4):
                hsl = slice((g % 2) * 64, (g % 2) * 64 + 64)
                for ri in range(2):
                    c0 = ((g // 2) * 2 + ri) * 64
                    I("pe", "matmul", [B_f], B_ps[1], psb[1][hsl, c0:c0 + 64],
                      cd_sb[:, ri, :], wf_sb[:, g, :], start=True, stop=True)
            I("dve", "tensor_copy", B_ps[1], [B_ab], out=ABall[:].rearrange("p q r e -> p (q r e)"), in_=psb[1][:, 0:256])
            Efull = sb(es, "Efull", [128, 90, 64], F32)
            Ebf = sb(es, "Ebf", [128, 6, 15, 64], BF16)
            cm_sb = sb(es, "cm_sb", [128, 64], F32)
            B_e = P.buf("efull")
            src = rpb_h.ap()[l].rearrange("r k q -> k r q")
            DMAs("sp", [(Efull[0:64], src), (Efull[64:128], src), (cm_sb[:], cm_h.ap())], [], [B_e])
            I("act", "activation", [B_e], [B_e], out=Efull[:], in_=Efull[:], func=AF.Exp)
            cmb = bass.AP(tensor=cm_sb, offset=0, ap=[[64, 128], [0, 90], [1, 64]])
            I("dve", "tensor_tensor", [B_e], [B_e], out=Ebf[:].rearrange("p h r q -> p (h r) q"), in0=Efull[:], in1=cmb, op=ALU.mult)
            n = 0
            for a in range(2):
                pa = slice(a * 64, (a + 1) * 64)
                for b in range(2):
                    for jj in range(5):
                        dr = 2 * (jj - 2) + a - b
                        eng = "dve"
                        n += 1
                        if -4 <= dr <= 3:
                            I(eng, "tensor_copy", [B_e], B_tab[n], out=EI[pa, :, jj, b, :], in_=Ebf[pa, :, dr + 7, :])
                        else:
                            I(eng, "memset", [], B_tab[n], EI[pa, :, jj, b, :], 0.0)
                    for wm, off in enumerate((0, 2, 4, 6)):
                        for j in range(4):
                            dr = 2 * j + a - off - b
                            eng = "dve"
                            n += 1
                            I(eng, "tensor_copy", [B_e], B_tab[n], out=EB[pa, wm, :, j, b, :], in_=Ebf[pa, :, dr + 7, :])
            P.flush()

        for s in range(NSEQ):
            x_src = xT if l == 0 else xs
            with ExitStack() as es:
                xt = [sb(es, f"a_xt{i}", [128, 8, TT], F32) for i in range(2)]
                sq = sb(es, "a_sq", [128, 8, TT], BF16)
                rs = [sb(es, f"a_rs{i}", [128, TT], F32) for i in range(2)]
                tmp = [sb(es, f"a_tmp{i}", [128, TT], F32) for i in range(2)]
                hT = [sb(es, f"a_hT{i}", [128, 8, TT], BF16) for i in range(2)]
                sgb = [sb(es, f"a_sgb{i}", [128, 11, TT], BF16) for i in range(2)]
                sgx = [sb(es, f"a_sgx{i}", [128, 3, TT], F32) for i in range(2)]
                sgv = [sb(es, f"a_sgv{i}", [128, 4, 384], BF16) for i in range(2)]
                B_xt = P.buf("a_xt", 2)
                B_sq = P.buf("a_sq")
                B_rs = P.buf("a_rs", 2)
                B_tmp = P.buf("a_tmp", 2)
                B_hT = P.buf("a_hT", 16)
                B_sgb = P.buf("a_sgb", 22)
                B_sgx = P.buf("a_sgx", 6)
                B_sgv = P.buf("a_sgv", 8)
                B_pst = P.buf("a_pst", NTT * 3)
                pbank = 0
                ntmp = 0
                nev = 0
                def a_load(tt):
                    pp = tt % 2
                    tsl = slice(tt * TT, (tt + 1) * TT)
                    DMAs("sp", [(xt[pp][:, 0:4, :], x_src[s, 0:4, :, tsl].rearrange("c p t -> p c t")),
                                (xt[pp][:, 4:8, :], x_src[s, 4:8, :, tsl].rearrange("c p t -> p c t"))],
                         B_xs[s * NTT + tt], B_xt[pp])

                def a_norm_steps(tt):
                    pp = tt % 2
                    steps = []
                    steps.append(lambda: I("act", "activation", B_xt[pp], [B_sq], out=sq[:], in_=xt[pp][:], func=AF.Square))

                    def ssmm():
                        for c in range(8):
                            I("pe", "matmul", [B_sq, B_const], B_ps[7], psb[7], ones[:], sq[:, c, :], start=(c == 0), stop=(c == 7))
                    steps.append(ssmm)

                    def lnexp():
                        I("act", "activation", B_ps[7] + [B_const], B_rs[pp], out=rs[pp][:], in_=psb[7], func=AF.Ln, scale=1.0 / D, bias=epsb[:])
                        I("act", "activation", B_rs[pp], B_rs[pp], out=rs[pp][:], in_=rs[pp][:], func=AF.Exp, scale=-0.5)
                    steps.append(lnexp)
                    for c in range(8):
                        def hstep(c=c):
                            q = ntmp_[0] % 2
                            ntmp_[0] += 1
                            I("dve", "tensor_tensor", B_xt[pp] + B_rs[pp], B_tmp[q], out=tmp[q][:], in0=xt[pp][:, c, :], in1=rs[pp][:], op=ALU.mult)
                            I("act", "activation", B_tmp[q] + [B_mod], B_hT[pp * 8 + c], out=hT[pp][:, c, :], in_=tmp[q][:],
                              func=AF.Identity, scale=A1[:, c, s:s + 1], bias=mod[:, c, s:s + 1])
                        steps.append(hstep)
                    return steps

                ntmp_ = [0]
                a_load(0)
                a_load(1)
                for st in a_norm_steps(0):
                    st()
                for tt in range(NTT):
                    pp = tt % 2
                    tsl = slice(tt * TT, (tt + 1) * TT)
                    nsteps = a_norm_steps(tt + 1) if tt + 1 < NTT else []
                    sched = {0: [0], 2: [1], 3: [2]}
                    for c in range(8):
                        sched.setdefault(4 + c, []).append(3 + c)
                    hrd = B_hT[pp * 8:(pp + 1) * 8]
                    for ci in range(18):
                        if nsteps:
                            for si in sched.get(ci, []):
                                nsteps[si]()
                        bk = pbank
                        pbank = (pbank + 1) % 6
                        if ci < 14:
                            col0 = ci * 128
                            for k in range(8):
                                I("pe", "matmul", hrd + [B_win], B_ps[bk], psb[bk], w_in_sb[:, k, col0:col0 + 128], hT[pp][:, k, :],
                                  start=(k == 0), stop=(k == 7))
                            if ci < 2:
                                dst, Bd, kind = sgb[pp][:, ci, :], B_sgb[pp * 11 + ci], "copy"
                            elif ci < 5:
                                dst, Bd, kind = sgx[pp][:, ci - 2, :], B_sgx[pp * 3 + ci - 2], "copy"
                            elif ci < 8:
                                dst, Bd, kind = sgb[pp][:, 2 + ci - 5, :], B_sgb[pp * 11 + 2 + ci - 5], "gelu"
                            else:
                                dst, Bd, kind = sgb[pp][:, 5 + ci - 8, :], B_sgb[pp * 11 + 5 + ci - 8], "copy"
                            if kind == "gelu":
                                I("act", "activation", B_ps[bk], Bd, out=dst, in_=psb[bk], func=AF.Gelu_apprx_tanh)
                            else:
                                nev += 1
                                if nev % 3 == 0:
                                    I("act", "activation", B_ps[bk], Bd, out=dst, in_=psb[bk], func=AF.Identity)
                                else:
                                    I("dve", "tensor_copy", B_ps[bk], Bd, out=dst, in_=psb[bk])
                        else:
                            sub = ci - 14
                            for k in range(8):
                                I("pe", "matmul", hrd + [B_win], B_ps[bk], psb[bk][:, 0:384], hT[pp][:, k, sub * 128:(sub + 1) * 128],
                                  w_in_sb[:, k, 1792:2176], start=(k == 0), stop=(k == 7))
                            I("dve", "tensor_copy", B_ps[bk], B_sgv[pp * 4 + sub], out=sgv[pp][:, sub, :], in_=psb[bk][:, 0:384])
                    if tt + 2 < NTT:
                        a_load(tt + 2)
                    DMAs("sp", [(pU[s, :, :, tsl].rearrange("c p t -> p c t"), sgb[pp][:, 0:2, :]),
                                (pgl[s, :, :, tsl].rearrange("c p t -> p c t"), sgb[pp][:, 2:5, :]),
                                (pq[s, :, :, tsl].rearrange("c p t -> p c t"), sgb[pp][:, 5:8, :]),
                                (pk[s, :, :, tsl].rearrange("c p t -> p c t"), sgb[pp][:, 8:11, :])],
                         B_sgb[pp * 11:(pp + 1) * 11], B_pst[tt * 3])
                    DMA("sp", ppx[s, :, :, tsl].rearrange("c p t -> p c t"), sgx[pp][:], B_sgx[pp * 3:(pp + 1) * 3], B_pst[tt * 3 + 1])
                    DMA("sp", pv[s, tt * 4:(tt + 1) * 4, :, :].rearrange("t p c -> p t c"), sgv[pp][:], B_sgv[pp * 4:(pp + 1) * 4], B_pst[tt * 3 + 2])
                P.flush()

            with ExitStack() as es:
                Mc = sb(es, "f_Mc", [128, 64, 2, 64], BF16)
                F01 = sb(es, "f_F01", [128, 2, 128], BF16)
                UT = [sb(es, f"f_UT{i}", [128, S], BF16) for i in range(2)]
                XsL = [sb(es, f"f_Xs{i}", [128, 64, 2, 64], BF16) for i in range(2)]
                PsL = [sb(es, f"f_Ps{i}", [128, 2, 64, 64], BF16) for i in range(2)]
                yf = [sb(es, f"f_yf{i}", [128, S], BF16) for i in range(2)]
                B_fc = P.buf("f_c")
                B_UT = P.buf("f_UT", 2)
                B_XsL = [P.buf(f"f_Xs{i}", 32) for i in range(2)]
                B_PsL = [P.buf(f"f_Ps{i}", 32) for i in range(2)]
                B_yf = P.buf("f_yf", 32)
                McF = Mc[:].rearrange("p a b c -> p (a b c)")
                DMAs("pool", [(McF[h * 64:(h + 1) * 64, i * 2048:(i + 1) * 2048], Mc_h.ap()[:, i * 2048:(i + 1) * 2048])
                              for h in range(2) for i in range(4)]
                     + [(F01[h * 64:(h + 1) * 64], F_h.ap()) for h in range(2)], [], [B_fc])
                fst = {"nev": 0, "pbank": 0}
                HS = (slice(0, 64), slice(64, 128))

                def f_evac(bk, Bd, dst, src):
                    fst["nev"] += 1
                    if fst["nev"] % 2 == 0:
                        I("act", "activation", B_ps[bk], Bd, out=dst, in_=src, func=AF.Identity)
                    else:
                        I("dve", "tensor_copy", B_ps[bk], Bd, out=dst, in_=src)

                def f_banks():
                    bk = fst["pbank"]
                    fst["pbank"] = (bk + 2) % 8
                    return (bk, bk + 1)

                def f_s0(q):
                    Xs, B_Xs = XsL[q], B_XsL[q]
                    DMA("sp", UT[q][:], pU[s, q], B_p[s], B_UT[q])
                    for s2b in range(16):
                        bks = f_banks()
                        for i in range(4):
                            s2 = s2b * 4 + i
                            for h in range(2):
                                I("pe", "matmul", B_UT[q] + [B_ab], B_ps[bks[h]], psb[bks[h]][HS[h], i * 128:(i + 1) * 128],
                                  UT[q][HS[h], s2::64], ABall[HS[h], q, :, :].rearrange("p r e -> p (r e)"), start=True, stop=True)
                        for h in range(2):
                            f_evac(bks[h], B_Xs[s2b * 2 + h], Xs[HS[h], s2b * 4:(s2b + 1) * 4, :, :].rearrange("p a r e -> p (a r e)"),
                                   psb[bks[h]][HS[h], :])

                def f_s1(q):
                    Xs, B_Xs, Ps, B_Ps = XsL[q], B_XsL[q], PsL[q], B_PsL[q]
                    for eb in range(16):
                        bks = f_banks()
                        for i in range(4):
                            e_ = eb * 4 + i
                            for ri in range(2):
                                for h in range(2):
                                    I("pe", "matmul", [B_Xs, B_fc], B_ps[bks[h]], psb[bks[h]][HS[h], i * 128:(i + 1) * 128],
                                      Xs[HS[h], :, ri, e_], F01[HS[h], ri, :], start=(ri == 0), stop=(ri == 1))
                        for h in range(2):
                            f_evac(bks[h], B_Ps[eb * 2 + h], Ps[HS[h], :, :, eb * 4:(eb + 1) * 4].rearrange("p r k e -> p (r k) e"),
                                   psb[bks[h]][HS[h], :].rearrange("p (e r) -> p r e", e=4))

                def f_s3(q):
                    Ps, B_Ps = PsL[q], B_PsL[q]
                    for kb in range(8):
                        bks = f_banks()
                        for i in range(8):
                            k1 = kb * 8 + i
                            for ri in range(2):
                                for h in range(2):
                                    I("pe", "matmul", [B_Ps, B_fc], B_ps[bks[h]], psb[bks[h]][HS[h], i * 64:(i + 1) * 64],
                                      Ps[HS[h], ri, k1, :], Mc[HS[h], k1, ri, :], start=(ri == 0), stop=(ri == 1))
                        for h in range(2):
                            f_evac(bks[h], B_yf[q * 16 + kb * 2 + h], yf[q][HS[h], :].rearrange("p (k2 k1) -> p k1 k2", k1=64)[:, kb * 8:(kb + 1) * 8, :],
                                   psb[bks[h]][HS[h], :].rearrange("p (a b) -> p a b", a=8))
                    DMA("sp", yTd[s, q], yf[q][:], B_yf[q * 16:(q + 1) * 16], B_y[s])

                f_s0(0)
                f_s0(1)
                f_s1(0)
                f_s1(1)
                f_s3(0)
                f_s3(1)
                P.flush()

            with ExitStack() as es:
                W = [sb(es, f"l_W{i}", [128, S], F32) for i in range(7)]
                ubf = sb(es, "l_ubf", [128, S], BF16)
                gl = sb(es, "l_gl", [128, S], BF16)
                yl = sb(es, "l_yl", [128, S], BF16)
                B_W = [P.buf(f"l_W{i}", 8) for i in range(7)]
                B_ubf = P.buf("l_ubf")
                B_gl = P.buf("l_gl")
                B_yl = P.buf("l_yl")
                lst = {"pbank": 0}
                roles = [list(range(7))]
                for c in range(1, 3):
                    r = roles[-1]
                    roles.append([r[2], r[3], r[1], r[4], r[5], r[6], r[0]])

                def RW(c, k):
                    return W[roles[c][k]], B_W[roles[c][k]]

                def l_front(c):
                    (px, Bpx), (u, Bu) = RW(c, 0), RW(c, 1)
                    DMAs("sp", [(px[:, 0:2048], ppx[s, c, :, 0:2048]), (px[:, 2048:4096], ppx[s, c, :, 2048:4096])], B_p[s], [Bpx])
                    cw = lambda k: convw[:, l, c, k:k + 1]
                    I("act", "activation", [Bpx, Bs], [Bu], out=u[:], in_=px[:], func=AF.Identity, scale=cw(2), bias=convb[:, l, c:c + 1])
                    I("dve", "scalar_tensor_tensor", [Bpx, Bu, Bs], [Bu], out=u[:, 2:S], in0=px[:, 0:S - 2], scalar=cw(0), in1=u[:, 2:S], op0=ALU.mult, op1=ALU.add)
                    I("dve", "scalar_tensor_tensor", [Bpx, Bu, Bs], [Bu], out=u[:, 1:S], in0=px[:, 0:S - 1], scalar=cw(1), in1=u[:, 1:S], op0=ALU.mult, op1=ALU.add)
                    I("dve", "scalar_tensor_tensor", [Bpx, Bu, Bs], [Bu], out=u[:, 0:S - 1], in0=px[:, 1:S], scalar=cw(3), in1=u[:, 0:S - 1], op0=ALU.mult, op1=ALU.add)

                def l_cast(c):
                    (u, Bu) = RW(c, 1)
                    I("act", "activation", [Bu], [B_ubf], out=ubf[:], in_=u[:], func=AF.Identity)

                def l_sig(c, dr_):
                    (ra, Bra), (ib, Bib) = RW(c, 2 + 2 * dr_), RW(c, 3 + 2 * dr_)
                    (u, Bu) = RW(c, 1)
                    vi = (l * 2 + dr_) * 3 + c
                    for tt in range(NTT):
                        tsl = slice(tt * TT, (tt + 1) * TT)
                        for ax, (dstw, Bd) in enumerate(((ra, Bra), (ib, Bib))):
                            bk = lst["pbank"]
                            lst["pbank"] = (bk + 1) % 8
                            I("pe", "matmul", [B_ubf, B_wg], B_ps[bk], psb[bk], WG[:, dr_, ax, c, :], ubf[:, tsl], start=True, stop=True)
                            I("act", "activation", B_ps[bk] + [Bs], Bd[tt], out=dstw[:, tsl], in_=psb[bk], func=AF.Sigmoid,
                              bias=lruv[:, ax, vi:vi + 1])
                    I("dve", "tensor_tensor", [Bib, Bu], [Bib], out=ib[:], in0=ib[:], in1=u[:], op=ALU.mult)

                def l_rest(c, dr_):
                    (ra, Bra), (ib, Bib) = RW(c, 2 + 2 * dr_), RW(c, 3 + 2 * dr_)
                    (tmpb, Btmp) = RW(c, 0 if dr_ == 0 else 6)
                    vi = (l * 2 + dr_) * 3 + c
                    I("act", "activation", [Bra, B_const], [Bra], out=ra[:], in_=ra[:], func=AF.Exp, scale=clam[:, vi:vi + 1])
                    I("act", "activation", [Bra], [Btmp], out=tmpb[:], in_=ra[:], func=AF.Square)
                    I("act", "activation", [Btmp], [Btmp], out=tmpb[:], in_=tmpb[:], func=AF.Sqrt, scale=-1.0, bias=1.0)
                    I("pool", "tensor_tensor", [Bib, Btmp], [Bib], out=ib[:], in0=ib[:], in1=tmpb[:], op=ALU.mult)
                    if dr_ == 0:
                        I("dve", "tensor_tensor_scan", [Bra, Bib], [Btmp], out=tmpb[:], data0=ra[:], data1=ib[:], initial=0.0,
                          op0=ALU.mult, op1=ALU.add)
                    else:
                        I("dve", "tensor_tensor_scan", [Bra, Bib], [Btmp], out=tmpb[:, ::-1], data0=ra[:, ::-1], data1=ib[:, ::-1],
                          initial=0.0, op0=ALU.mult, op1=ALU.add)

                def l_final(c):
                    (h0, Bh0), (h1, Bh1) = RW(c, 0), RW(c, 6)
                    DMA("sp", gl[:], pgl[s, c], B_p[s], [B_gl])
                    I("dve", "tensor_tensor", [Bh0, Bh1], [Bh0], out=h0[:], in0=h0[:], in1=h1[:], op=ALU.add)
                    I("dve", "tensor_tensor", [Bh0, B_gl], [B_yl], out=yl[:], in0=h0[:], in1=gl[:], op=ALU.mult)
                    DMA("sp", yTd[s, 2 + c], yl[:], [B_yl], B_y[s])

                l_front(0)
                l_cast(0)
                for c in range(3):
                    l_sig(c, 0)
                    l_rest(c, 0)
                    l_sig(c, 1)
                    if c + 1 < 3:
                        l_front(c + 1)
                    l_rest(c, 1)
                    if c + 1 < 3:
                        l_cast(c + 1)
                    l_final(c)
                P.flush()

            with ExitStack() as es:
                qT = sb(es, "n_q", [128, S], BF16)
                kz = [sb(es, f"n_kz{i}", [128, S], BF16) for i in range(2)]
                vA = sb(es, "n_vA", [128, 32, 128], BF16)
                vB = sb(es, "n_vB", [128, 32, 128], BF16)
                yn = sb(es, "n_yn", [128, S], BF16)
                pe_ = [sb(es, f"n_pe{i}", [128, 2, 5, 128], BF16) for i in range(2)]
                pt_ = [sb(es, f"n_pt{i}", [128, 2, 5, 128], BF16) for i in range(3)]
                lnd = [sb(es, f"n_lnd{i}", [128, 128], F32) for i in range(2)]
                B_q = P.buf("n_q")
                B_k = P.buf("n_k")
                B_v = P.buf("n_v")
                B_yn = P.buf("n_yn", 32)
                B_pe = P.buf("n_pe", 6)
                B_pt = P.buf("n_pt", 3)
                B_lnd = P.buf("n_lnd", 2)
                I("pool", "memset", [], [B_v], vA[:], 0.0)
                I("pool", "memset", [], [B_v], vB[:], 0.0)
                I("pool", "memset", [], [B_k], kz[0][:], 0.0)
                I("pool", "memset", [], [B_k], kz[1][:], 0.0)
                nb = 0
                for hp in range(3):
                    DMA("sp", qT[:], pq[s, hp], B_p[s], [B_q])
                    DMAs("sp", [(kz[0][0:64, :], pk[s, hp, 0:64, :]), (kz[1][64:128, :], pk[s, hp, 64:128, :])], B_p[s] + [B_k], [B_k])
                    for half in range(2):
                        hs = slice(half * 16, (half + 1) * 16)
                        DMAs("sp", [(vA[:, hs, 0:64], pv[s, hs, :, hp * 128:hp * 128 + 64].rearrange("t p c -> p t c")),
                                    (vB[:, hs, 64:128], pv[s, hs, :, hp * 128 + 64:hp * 128 + 128].rearrange("t p c -> p t c"))],
                             B_p[s] + [B_v], [B_v])

                    def blk_info(m):
                        if m < 2:
                            return list(range(4)), EB[:, m, 2 * hp:2 * hp + 2].rearrange("p h j b q -> p h (j b q)")
                        if m >= 30:
                            return list(range(28, 32)), EB[:, m - 28, 2 * hp:2 * hp + 2].rearrange("p h j b q -> p h (j b q)")
                        return list(range(m - 2, m + 3)), EI[:, 2 * hp:2 * hp + 2].rearrange("p h j b q -> p h (j b q)")

                    def scores(m, par):
                        js, tab = blk_info(m)
                        nj = len(js)
                        b0 = 3 * par
                        qs = slice(m * 128, (m + 1) * 128)
                        for hh in range(2):
                            hsl = slice(hh * 64, (hh + 1) * 64)
                            for jj, j in enumerate(js[:4]):
                                I("pe", "matmul", [B_q, B_k], B_ps[b0 + hh], psb[b0 + hh][:, jj * 128:(jj + 1) * 128],
                                  kz[hh][:, j * 128:(j + 1) * 128], qT[:, qs], start=True, stop=True)
                            if nj == 5:
                                j = js[4]
                                I("pe", "matmul", [B_q, B_k], B_ps[b0 + 2], psb[b0 + 2][:, hh * 128:(hh + 1) * 128],
                                  kz[hh][:, j * 128:(j + 1) * 128], qT[:, qs], start=True, stop=True)
                        for hh in range(2):
                            I("act", "activation", B_ps[b0 + hh], B_pe[par * 3 + hh], out=pe_[par][:, hh, 0:4, :].rearrange("p j q -> p (j q)"),
                              in_=psb[b0 + hh][:], func=AF.Exp, scale=0.125)
                        if nj == 5:
                            I("act", "activation", B_ps[b0 + 2], B_pe[par * 3 + 2], out=pe_[par][:, :, 4, :],
                              in_=psb[b0 + 2][:, 0:256].rearrange("p (h q) -> p h q", h=2), func=AF.Exp, scale=0.125)
                        nq = nj * 128
                        I("dve", "tensor_tensor", B_pe[par * 3:(par + 1) * 3] + [B_tab], B_pt[m % 3],
                          out=pt_[m % 3][:].rearrange("p h j q -> p h (j q)")[:, :, 0:nq],
                          in0=pe_[par][:].rearrange("p h j q -> p h (j q)")[:, :, 0:nq], in1=tab, op=ALU.mult)

                    def pv_(m, par):
                        js, tab = blk_info(m)
                        nj = len(js)
                        tot = 2 * nj
                        n = 0
                        for hh in range(2):
                            vv = vA if hh == 0 else vB
                            oo = onesA if hh == 0 else onesB
                            for jj, j in enumerate(js):
                                hsl = slice(hh * 64, (hh + 1) * 64)
                                I("pe", "matmul", B_pt[m % 3] + [B_v], B_ps[6], psb[6][hsl, 0:128], vv[:, j, hsl], pt_[m % 3][:, hh, jj, :],
                                  start=(jj == 0), stop=(jj == nj - 1))
                                I("pe", "matmul", B_pt[m % 3] + [B_const], B_ps[7], psb[7][hsl, 0:128], ones[:, 0:64], pt_[m % 3][:, hh, jj, :],
                                  start=(jj == 0), stop=(jj == nj - 1))
                                n += 1
                        I("act", "activation", B_ps[7], B_lnd[par], out=lnd[par][:], in_=psb[7][:, 0:128], func=AF.Ln)
                        I("act", "activation", B_lnd[par], B_lnd[par], out=lnd[par][:], in_=lnd[par][:], func=AF.Exp, scale=-1.0)
                        I("dve", "tensor_tensor", B_ps[6] + B_lnd[par], B_yn[m], out=yn[:, m * 128:(m + 1) * 128], in0=psb[6][:, 0:128],
                          in1=lnd[par][:], op=ALU.mult)

                    scores(0, 0)
                    scores(1, 1)
                    for m in range(32):
                        if m + 2 < 32:
                            scores(m + 2, m % 2)
                        pv_(m, m % 2)
                        nb += 1
                    DMA("sp", yTd[s, 5 + hp], yn[:], [B_yn], B_y[s])
                P.flush()

            with ExitStack() as es:
                wo = sb(es, "c_wo", [128, 8, D], BF16)
                yt = [sb(es, f"c_yt{i}", [128, 8, TT], BF16) for i in range(2)]
                xt = [sb(es, f"c_xt{i}", [128, 8, TT], F32) for i in range(3)]
                B_wo = P.buf("c_wo", 8)
                B_yt = P.buf("c_yt", 2)
                B_xt = P.buf("c_xt", 24)
                sq = sb(es, "c_sq", [128, 8, TT], BF16)
                rs = [sb(es, f"c_rs{i}", [128, TT], F32) for i in range(2)]
                tmp = [sb(es, f"c_tmp{i}", [128, TT], F32) for i in range(2)]
                h2 = [sb(es, f"c_h2{i}", [128, 8, TT], BF16) for i in range(2)]
                B_sq = P.buf("c_sq")
                B_rs = P.buf("c_rs", 2)
                B_tmp = P.buf("c_tmp", 2)
                B_h2 = P.buf("c_h2", 16)
                ntmp = 0
                for k in range(8):
                    DMAs("pool", [(wo[:, k, 0:512], w_out[l, k * 128:(k + 1) * 128, 0:512]),
                                  (wo[:, k, 512:1024], w_out[l, k * 128:(k + 1) * 128, 512:1024])], [], B_wo[k])
                pbank = 0
                def c_load(tt):
                    pp = tt % 2
                    p3 = tt % 3
                    tsl = slice(tt * TT, (tt + 1) * TT)
                    DMA("sp", yt[pp][:], yTd[s, :, :, tsl].rearrange("c p t -> p c t"), B_y[s], B_yt[pp])
                    DMAs("sp", [(xt[p3][:, 0:4, :], x_src[s, 0:4, :, tsl].rearrange("c p t -> p c t")),
                                (xt[p3][:, 4:8, :], x_src[s, 4:8, :, tsl].rearrange("c p t -> p c t"))],
                         B_xs[s * NTT + tt], B_xt[p3 * 8:(p3 + 1) * 8])

                def c_norm_steps(tt):
                    pp = tt % 2
                    p3 = tt % 3
                    tsl = slice(tt * TT, (tt + 1) * TT)
                    Bx = B_xt[p3 * 8:(p3 + 1) * 8]
                    steps = []
                    steps.append(lambda: I("act", "activation", Bx, [B_sq], out=sq[:], in_=xt[p3][:], func=AF.Square))

                    def ssmm():
                        for c in range(8):
                            I("pe", "matmul", [B_sq, B_const], B_ps[7], psb[7], ones[:], sq[:, c, :], start=(c == 0), stop=(c == 7))
                    steps.append(ssmm)

                    def lnexp():
                        I("act", "activation", B_ps[7] + [B_const], B_rs[pp], out=rs[pp][:], in_=psb[7], func=AF.Ln, scale=1.0 / D, bias=epsb[:])
                        I("act", "activation", B_rs[pp], B_rs[pp], out=rs[pp][:], in_=rs[pp][:], func=AF.Exp, scale=-0.5)
                    steps.append(lnexp)
                    for c in range(8):
                        def hstep(c=c):
                            q = ntmp_[0] % 2
                            ntmp_[0] += 1
                            I("dve", "tensor_tensor", B_xt[p3 * 8 + c] + B_rs[pp], B_tmp[q], out=tmp[q][:], in0=xt[p3][:, c, :], in1=rs[pp][:], op=ALU.mult)
                            I("act", "activation", B_tmp[q] + [B_mod], B_h2[pp * 8 + c], out=h2[pp][:, c, :], in_=tmp[q][:],
                              func=AF.Identity, scale=A2[:, c, s:s + 1], bias=mod[:, 24 + c, s:s + 1])
                        steps.append(hstep)
                    steps.append(lambda: DMA("sp", h2d[s, :, :, tsl].rearrange("c p t -> p c t"), h2[pp][:], B_h2[pp * 8:(pp + 1) * 8], B_h2d[s * NTT + tt]))
                    return steps

                ntmp_ = [0]
                c_sched = {0: [0], 2: [1], 3: [2, 3, 4], 4: [5, 6], 5: [7, 8], 6: [9, 10], 7: [11]}
                c_load(0)
                for tt in range(NTT):
                    pp = tt % 2
                    p3 = tt % 3
                    tsl = slice(tt * TT, (tt + 1) * TT)
                    if tt + 1 < NTT:
                        c_load(tt + 1)
                    nsteps = c_norm_steps(tt - 1) if tt >= 1 else []
                    for m in range(8):
                        if nsteps:
                            for si in c_sched.get(m, []):
                                nsteps[si]()
                        bk = pbank
                        pbank = (pbank + 1) % 7
                        for k in range(8):
                            I("pe", "matmul", B_yt[pp] + [B_wo], B_ps[bk], psb[bk], wo[:, k, m * 128:(m + 1) * 128], yt[pp][:, k, :],
                              start=(k == 0), stop=(k == 7))
                        I("dve", "scalar_tensor_tensor", B_ps[bk] + B_xt[p3 * 8 + m] + [B_mod], B_xt[p3 * 8 + m], out=xt[p3][:, m, :],
                          in0=psb[bk], scalar=mod[:, 16 + m, s:s + 1], in1=xt[p3][:, m, :], op0=ALU.mult, op1=ALU.add)
                    DMAs("sp", [(xs[s, 0:4, :, tsl].rearrange("c p t -> p c t"), xt[p3][:, 0:4, :]),
                                (xs[s, 4:8, :, tsl].rearrange("c p t -> p c t"), xt[p3][:, 4:8, :])],
                         B_xt[p3 * 8:(p3 + 1) * 8], B_xs[s * NTT + tt])
                for st in c_norm_steps(NTT - 1):
                    st()
                P.flush()

        es_layer.close()
        with ExitStack() as es:
            wg = sb(es, "d_wg", [128, 8, D_FF], BF16)
            wu = sb(es, "d_wu", [128, 8, D_FF], BF16)
            wd = sb(es, "d_wd", [128, NFF, D], BF16)
            xt = sb(es, "d_xt", [128, 8, TT], F32)
            h2 = [sb(es, f"d_h2{i}", [128, 8, TT], BF16) for i in range(2)]
            act = sb(es, "d_act", [128, NFF, TT], BF16)
            sg = [sb(es, f"d_sg{i}", [128, TT], BF16) for i in range(2)]
            last = (l == NL - 1)
            if last:
                fsq = sb(es, "d_fsq", [128, 8, TT], BF16)
                frs = sb(es, "d_frs", [128, TT], F32)
                B_fsq = P.buf("d_fsq")
                B_frs = P.buf("d_frs")
            B_wgu = P.buf("d_wgu", NFF)
            B_wd = P.buf("d_wd", NFF)
            B_xt = P.buf("d_xt", 8)
            B_h2 = P.buf("d_h2", 2)
            B_act = P.buf("d_act", NFF)
            B_sg = P.buf("d_sg", 2)
            for j in range(NFF):
                cs_ = slice(j * 128, (j + 1) * 128)
                DMAs("pool", [(wg[:, :, cs_], w_gate[l, :, cs_].rearrange("(k p) c -> p k c", p=128)),
                              (wu[:, :, cs_], w_up[l, :, cs_].rearrange("(k p) c -> p k c", p=128))], [], B_wgu[j])
            for j in range(NFF):
                DMAs("pool", [(wd[:, j, 0:512], w_down[l, j * 128:(j + 1) * 128, 0:512]),
                              (wd[:, j, 512:1024], w_down[l, j * 128:(j + 1) * 128, 512:1024])], [], B_wd[j])
            nsg = 0

            def d_load(n):
                s_, tt_ = divmod(n, NTT)
                DMA("sp", h2[n % 2][:], h2d[s_, :, :, tt_ * TT:(tt_ + 1) * TT].rearrange("c p t -> p c t"),
                    B_h2d[s_ * NTT + tt_], B_h2[n % 2])

            d_load(0)
            n = 0
            pending = []

            def x_loads(s_, tt_):
                tsl_ = slice(tt_ * TT, (tt_ + 1) * TT)
                DMAs("sp", [(xt[:, c, :], xs[s_, c, :, tsl_]) for c in range(4)], B_xs[s_ * NTT + tt_], B_xt[0:4])
                DMAs("sp", [(xt[:, c, :], xs[s_, c, :, tsl_]) for c in range(4, 8)], B_xs[s_ * NTT + tt_], B_xt[4:8])

            def fin_steps(s_, tt_):
                tsl_ = slice(tt_ * TT, (tt_ + 1) * TT)
                st = []
                st.append(lambda: I("act", "activation", [B_xt], [B_fsq], out=fsq[:], in_=xt[:], func=AF.Square))

                def ssmm():
                    for c in range(8):
                        I("pe", "matmul", [B_fsq, B_const], B_ps[4], psb[4], ones[:], fsq[:, c, :], start=(c == 0), stop=(c == 7))
                st.append(ssmm)

                def lnexp():
                    I("act", "activation", B_ps[4] + [B_const], [B_frs], out=frs[:], in_=psb[4], func=AF.Ln, scale=1.0 / D, bias=epsb[:])
                    I("act", "activation", [B_frs], [B_frs], out=frs[:], in_=frs[:], func=AF.Exp, scale=-0.5)
                st.append(lnexp)

                def outs():
                    for c in range(8):
                        I("dve", "scalar_tensor_tensor", B_xt[c] + [B_frs, Bs], B_xt[c], out=xt[:, c, :], in0=xt[:, c, :],
                          scalar=gfin[:, c:c + 1], in1=frs[:], op0=ALU.mult, op1=ALU.mult)
                        DMA("sp", outT[s_, c, :, tsl_], xt[:, c, :], B_xt[c], [B_out])
                st.append(outs)
                return st

            for s in range(NSEQ):
                for tt in range(NTT):
                    tsl = slice(tt * TT, (tt + 1) * TT)
                    hp_ = n % 2
                    n += 1
                    if n < NSEQ * NTT:
                        d_load(n)
                    if not pending:
                        x_loads(s, tt)
                    f_sched = {0: 0, 2: 1, 3: 2, 4: 3}
                    for j in range(NFF):
                        if pending and j in f_sched:
                            pending[f_sched[j]]()
                            if j == 4:
                                pending = []
                                x_loads(s, tt)
                        bg = j % 2
                        bu = 2 + j % 2
                        for k in range(8):
                            I("pe", "matmul", B_h2[hp_] + B_wgu[j], B_ps[bg], psb[bg][:], wg[:, k, j * 128:(j + 1) * 128], h2[hp_][:, k, :],
                              start=(k == 0), stop=(k == 7))
                        for k in range(8):
                            I("pe", "matmul", B_h2[hp_] + B_wgu[j], B_ps[bu], psb[bu][:], wu[:, k, j * 128:(j + 1) * 128], h2[hp_][:, k, :],
                              start=(k == 0), stop=(k == 7))
                        q = nsg % 2
                        nsg += 1
                        I("act", "activation", B_ps[bg], B_sg[q], out=sg[q][:], in_=psb[bg][:], func=AF.Silu)
                        I("dve", "tensor_tensor", B_ps[bu] + B_sg[q], B_act[j], out=act[:, j, :], in0=psb[bu][:], in1=sg[q][:], op=ALU.mult)
                    for m in range(8):
                        bk = 4 + m % 4
                        for j in range(NFF):
                            I("pe", "matmul", B_act[j] + B_wd[j], B_ps[bk], psb[bk][:], wd[:, j, m * 128:(m + 1) * 128], act[:, j, :],
                              start=(j == 0), stop=(j == NFF - 1))
                        I("dve", "scalar_tensor_tensor", B_ps[bk] + B_xt[m] + [B_mod], B_xt[m], out=xt[:, m, :],
                          in0=psb[bk][:], scalar=mod[:, 40 + m, s:s + 1], in1=xt[:, m, :], op0=ALU.mult, op1=ALU.add)
                        if not last:
                            DMA("sp", xs[s, m, :, tsl], xt[:, m, :], B_xt[m], B_xs[s * NTT + tt])
                    if last:
                        pending = fin_steps(s, tt)
            for st_ in pending:
                st_()
            P.flush()

    P.op("sp", None, reads=[B_out])
    P.flush()
    es_glob.close()
    build.stats = dict(P.total)
    return nc


def _prep_shared(inp, NL):
    f = lambda a: np.ascontiguousarray(np.asarray(a, dtype=np.float32))
    cd, Fm, Mc, cm = _consts()
    sh = {}
    sh["w_ada"] = f(inp["w_ada"][:NL])
    sh["b_adaT"] = f(np.asarray(inp["b_ada"])[:NL].reshape(NL, 48, 128).transpose(0, 2, 1))
    g = np.stack([np.asarray(inp["g_mix"])[:NL], np.asarray(inp["g_ffn"])[:NL]], axis=1)
    sh["gT"] = f(g.reshape(NL, 2, 8, 128).transpose(3, 0, 1, 2))
    sh["gfin"] = f(np.asarray(inp["g_final"]).reshape(8, 128).T)
    sh["w_in"] = f(inp["w_in"][:NL])
    sh["w_out"] = f(inp["w_out"][:NL])
    sh["w_gate"] = f(inp["w_ffn_gate"][:NL])
    sh["w_up"] = f(inp["w_ffn_up"][:NL])
    sh["w_down"] = f(inp["w_ffn_down"][:NL])
    sh["w_four"] = f(inp["w_fourier"][:NL])
    sh["convw"] = f(np.asarray(inp["conv_w"])[:NL].reshape(NL, 4, 3, 128).transpose(3, 0, 2, 1))
    sh["convb"] = f(np.asarray(inp["conv_b"])[:NL].reshape(NL, 3, 128).transpose(2, 0, 1))
    sh["lru_wa"] = f(inp["lru_w_a"][:NL])
    sh["lru_wx"] = f(inp["lru_w_x"][:NL])
    v = np.stack([np.asarray(inp["lru_b_a"])[:NL], np.asarray(inp["lru_b_x"])[:NL],
                  np.asarray(inp["lru_lambda"])[:NL]], axis=0)
    sh["lruv"] = f(v.reshape(3, NL, 2, 3, 128).transpose(4, 0, 1, 2, 3))
    kc = np.arange(64)[:, None]
    qc = np.arange(64)[None, :]
    idx = kc - qc + 15
    valid = (idx >= 0) & (idx <= 30)
    rp = np.asarray(inp["na_rpb"], np.float32)[:NL][..., np.clip(idx, 0, 30)]
    rp = np.where(valid, rp, np.float32(0.0))
    sh["rpbtab"] = f(rp.reshape(NL, 90, 64, 64))
    sh["c_cd"] = f(cd)
    sh["c_F"] = f(Fm)
    sh["c_Mc"] = f(Mc)
    sh["c_cm"] = f(cm)
    return sh


def _prep_core(x, c, NSEQ):
    xT = np.ascontiguousarray(np.asarray(x, np.float32).transpose(0, 2, 1).reshape(NSEQ, 8, 128, S))
    cT = np.zeros((128, 8, 2), np.float32)
    cT[:, :, :NSEQ] = np.asarray(c, np.float32).reshape(NSEQ, 8, 128).transpose(2, 1, 0)
    return {"xT": xT, "cT": cT}


def run(inp, NSEQ, NL, n_cores, stop=None, maxops=None):
    nc = build(NSEQ, NL, stop, maxops)
    sh = _prep_shared(inp, NL)
    x = np.asarray(inp["x"])
    c = np.asarray(inp["c"])
    in_maps = []
    for i in range(n_cores):
        m = dict(sh)
        m.update(_prep_core(x[i * NSEQ:(i + 1) * NSEQ], c[i * NSEQ:(i + 1) * NSEQ], NSEQ))
        in_maps.append(m)
    res = run_bass_kernel_spmd(nc, in_maps, core_ids=list(range(n_cores)))
    outs = []
    for i in range(n_cores):
        o = np.asarray(res.results[i]["outT"]).reshape(NSEQ, D, S).transpose(0, 2, 1)
        outs.append(o)
    return np.ascontiguousarray(np.concatenate(outs, axis=0).astype(np.float32))


def kernel(**inputs):
    return run(inputs, 2, DEPTH, N_CORES)
```

```python
import math
from contextlib import ExitStack
import numpy as np
import concourse.bass as bass
import concourse.mybir as mybir
from concourse.bass_utils import run_bass_kernel_spmd

F32 = mybir.dt.float32
BF16 = mybir.dt.bfloat16
AF = mybir.ActivationFunctionType
ALU = mybir.AluOpType

EPOCH = 30000
SAME_ENGINE_SYNC = True

D = 1024
S = 4096
DEPTH = 4
D_IN = 2176
D_FF = 2816
NFF = 22
TT = 512
NTT = 8
EPS = 1e-6
N_CORES = 8


class Buf:
    def __init__(self, prog, name, n=1):
        self.name = name
        self.n = n
        self.lw = [None] * n
        self.rd = [[] for _ in range(n)]
        prog.bufs.append(self)

    def __getitem__(self, idx):
        if isinstance(idx, slice):
            return [(self, i) for i in range(*idx.indices(self.n))]
        if isinstance(idx, (list, tuple)):
            return [(self, i) for i in idx]
        return [(self, idx)]

    def all(self):
        return [(self, i) for i in range(self.n)]

    def reset(self):
        self.lw = [None] * self.n
        self.rd = [[] for _ in range(self.n)]


def _flat(lst):
    out = []
    for x in lst:
        if isinstance(x, Buf):
            out.extend(x.all())
        elif isinstance(x, list):
            out.extend(_flat(x))
        else:
            out.append(x)
    return out


class Op:
    __slots__ = ("eng", "fn", "chan", "seq", "waits", "signal", "tick", "kdone",
                 "is_dma", "ndma", "idx")


class Prog:
    ENGS = ("pe", "act", "dve", "pool", "sp")

    def __init__(self, nc, n_dma_sems=6):
        self.nc = nc
        self.n_dma_sems = n_dma_sems
        self.bufs = []
        self.tick = {e: 0 for e in self.ENGS}
        self.dtick = {}
        self.sems = {}
        self.nops = 0
        self.total = {e: 0 for e in self.ENGS}
        self.nflush = 0
        self.stop = 10 ** 9
        self.maxops = 10 ** 9
        self.verbose = False
        self._reset()

    def _reset(self):
        self.ops = {e: [] for e in self.ENGS}
        self.know = {e: {} for e in self.ENGS}
        self.chan_seq = {}
        self.chan_last = {}
        self.dma_rr = {e: 0 for e in self.ENGS}
        for b in self.bufs:
            b.reset()

    def buf(self, name, n=1):
        return Buf(self, name, n)

    def op(self, eng, fn, reads=(), writes=(), dma=0):
        if self.nflush >= self.stop or self.nops >= self.maxops:
            return None
        X = Op()
        X.eng = eng
        X.fn = fn
        X.is_dma = dma > 0
        X.ndma = dma
        X.signal = X.is_dma
        X.tick = None
        X.idx = self.nops
        self.nops += 1
        reads = _flat(reads)
        writes = _flat(writes)
        if X.is_dma:
            k = self.dma_rr[eng]
            self.dma_rr[eng] = (k + 1) % self.n_dma_sems
            X.chan = ("dma", eng, k)
        else:
            X.chan = eng
        X.seq = self.chan_seq.get(X.chan, 0) + 1
        self.chan_seq[X.chan] = X.seq
        deps = {}
        for (b, i) in reads:
            w = b.lw[i]
            if w is not None:
                deps[w.idx] = (w, True)
        for (b, i) in writes:
            w = b.lw[i]
            if w is not None:
                deps[w.idx] = (w, True)
            for r in b.rd[i]:
                if r.idx not in deps:
                    deps[r.idx] = (r, False)
        if X.is_dma:
            p = self.chan_last.get(X.chan)
            if p is not None:
                deps[p.idx] = (p, True)
            self.chan_last[X.chan] = X
        K = self.know[eng]
        waits = []
        for di in sorted(deps.keys(), reverse=True):
            d, hard = deps[di]
            if (not d.is_dma) and (not X.is_dma) and d.eng == eng:
                if eng == "pe" or not hard or not SAME_ENGINE_SYNC:
                    continue
            if K.get(d.chan, 0) >= d.seq:
                continue
            d.signal = True
            waits.append(d)
            for c, s in d.kdone.items():
                if K.get(c, 0) < s:
                    K[c] = s
        X.waits = waits
        kd = dict(K)
        kd[X.chan] = X.seq
        X.kdone = kd
        for (b, i) in reads:
            b.rd[i].append(X)
        for (b, i) in writes:
            b.lw[i] = X
            b.rd[i] = []
        self.ops[eng].append(X)
        return X

    def _sem(self, key):
        s = self.sems.get(key)
        if s is None:
            s = self.nc.alloc_semaphore("s_" + "_".join(str(k) for k in key))
            self.sems[key] = s
        return s

    def flush(self):
        self.nflush += 1
        if self.nflush > self.stop:
            return
        if self.verbose:
            print("flush", self.nflush, "nops", self.nops, {e: len(v) for e, v in self.ops.items()}, flush=True)
        last_dmas = list(self.chan_last.values())
        if last_dmas:
            fb = Buf(self, "_fence")
            for X in last_dmas:
                fb.rd[0].append(X)
            self.op("sp", None, writes=[fb])
            self.bufs.remove(fb)
        for e in self.ENGS:
            for X in self.ops[e]:
                if X.is_dma:
                    c = self.dtick.get(X.chan, 0) + X.ndma
                    self.dtick[X.chan] = c
                    X.tick = c * 16
                elif X.signal:
                    self.tick[e] += 1
                    X.tick = self.tick[e]
            self.total[e] += len(self.ops[e])

        def sem_val(d):
            if d.is_dma:
                return self._sem(d.chan), d.tick
            ep = (d.tick - 1) // EPOCH
            return self._sem((d.eng, ep)), d.tick - ep * EPOCH

        def run(engobj, lst):
            for X in lst:
                for d in reversed(X.waits):
                    s, v = sem_val(d)
                    engobj.wait_ge(s, v)
                if X.fn is None:
                    continue
                if X.is_dma:
                    s = self._sem(X.chan)
                    instrs = X.fn(engobj)
                    if not isinstance(instrs, (list, tuple)):
                        instrs = [instrs]
                    assert len(instrs) == X.ndma, (len(instrs), X.ndma)
                    for ins in instrs:
                        ins.then_inc(s, 16)
                else:
                    ins = X.fn(engobj)
                    if X.signal:
                        s, _ = sem_val(X)
                        ins.then_inc(s, 1)

        ops = self.ops
        with self.nc.Block() as block:
            @block.tensor
            def _(eng):
                run(eng, ops["pe"])

            @block.scalar
            def _(eng):
                run(eng, ops["act"])

            @block.vector
            def _(eng):
                run(eng, ops["dve"])

            @block.gpsimd
            def _(eng):
                run(eng, ops["pool"])

            @block.sync
            def _(eng):
                run(eng, ops["sp"])
        self._reset()


def _consts():
    d = np.arange(64)
    ang = 2.0 * np.pi * np.outer(d, d) / 64.0
    C64 = np.cos(ang)
    S64 = np.sin(ang)
    cd = np.stack([C64 / 512.0, -S64 / 512.0], axis=1).astype(np.float32)
    F = np.zeros((64, 2, 2, 64), np.float64)
    F[:, 0, 0, :] = C64
    F[:, 0, 1, :] = -S64
    F[:, 1, 0, :] = S64
    F[:, 1, 1, :] = C64
    F = F.reshape(64, 2, 128).astype(np.float32)
    s2 = np.arange(64)[:, None, None]
    k1 = np.arange(64)[None, :, None]
    k2 = np.arange(64)[None, None, :]
    th = 2.0 * np.pi * ((s2 * (k1 + 64 * k2)) % 4096) / 4096.0
    Mc = np.stack([np.cos(th), np.sin(th)], axis=2).astype(np.float32)
    qc = np.arange(64)
    cs = np.clip(qc - 8, 0, 48)
    kc = np.arange(64)[:, None]
    cm = ((kc >= cs[None, :]) & (kc < cs[None, :] + 16)).astype(np.float32)
    cm = np.concatenate([cm, cm], axis=0)
    return cd, F, Mc.reshape(64, 64 * 2 * 64), cm


def build(NSEQ=2, NL=DEPTH, stop=None, maxops=None):
    nc = bass.Bass("TRN2", target_bir_lowering=False)
    P = Prog(nc)
    if stop is not None:
        P.stop = stop
    if maxops is not None:
        P.maxops = maxops

    def dram(name, shape, dt, kind="ExternalInput"):
        return nc.dram_tensor(name, list(shape), dt, kind=kind)

    xT_h = dram("xT", [NSEQ, 8, 128, S], F32)
    cT_h = dram("cT", [128, 8, 2], F32)
    w_ada_h = dram("w_ada", [NL, D, 6 * D], F32)
    b_adaT_h = dram("b_adaT", [NL, 128, 48], F32)
    gT_h = dram("gT", [128, NL, 2, 8], F32)
    gfin_h = dram("gfin", [128, 8], F32)
    w_in_h = dram("w_in", [NL, D, D_IN], F32)
    w_out_h = dram("w_out", [NL, D, D], F32)
    w_gate_h = dram("w_gate", [NL, D, D_FF], F32)
    w_up_h = dram("w_up", [NL, D, D_FF], F32)
    w_down_h = dram("w_down", [NL, D_FF, D], F32)
    w_four_h = dram("w_four", [NL, 4, 64, 64], F32)
    convw_h = dram("convw", [128, NL, 3, 4], F32)
    convb_h = dram("convb", [128, NL, 3], F32)
    lru_wa_h = dram("lru_wa", [NL, 2, 6, 64, 64], F32)
    lru_wx_h = dram("lru_wx", [NL, 2, 6, 64, 64], F32)
    lruv_h = dram("lruv", [128, 3, NL, 2, 3], F32)
    rpb_h = dram("rpbtab", [NL, 90, 64, 64], F32)
    cd_h = dram("c_cd", [64, 2, 64], F32)
    F_h = dram("c_F", [64, 2, 128], F32)
    Mc_h = dram("c_Mc", [64, 8192], F32)
    cm_h = dram("c_cm", [128, 64], F32)
    out_h = dram("outT", [NSEQ, 8, 128, S], F32, kind="ExternalOutput")
    xs_h = dram("xs", [NSEQ, 8, 128, S], F32, kind="Internal")
    pU_h = dram("pU", [NSEQ, 2, 128, S], BF16, kind="Internal")
    pgl_h = dram("pgl", [NSEQ, 3, 128, S], BF16, kind="Internal")
    pq_h = dram("pq", [NSEQ, 3, 128, S], BF16, kind="Internal")
    pk_h = dram("pk", [NSEQ, 3, 128, S], BF16, kind="Internal")
    ppx_h = dram("ppx", [NSEQ, 3, 128, S], F32, kind="Internal")
    pv_h = dram("pv", [NSEQ, 32, 128, 384], BF16, kind="Internal")
    yT_h = dram("yTd", [NSEQ, 8, 128, S], BF16, kind="Internal")
    h2_h = dram("h2d", [NSEQ, 8, 128, S], BF16, kind="Internal")

    xT, w_ada, w_in, w_out = xT_h.ap(), w_ada_h.ap(), w_in_h.ap(), w_out_h.ap()
    w_gate, w_up, w_down = w_gate_h.ap(), w_up_h.ap(), w_down_h.ap()
    xs, pU, pgl, pq, pk, ppx, pv, yTd = (h.ap() for h in (xs_h, pU_h, pgl_h, pq_h, pk_h, ppx_h, pv_h, yT_h))
    outT = out_h.ap()
    h2d = h2_h.ap()

    B_xs = P.buf("xs", NSEQ * NTT)
    B_p = P.buf("p", NSEQ)
    B_y = P.buf("yd", NSEQ)
    B_h2d = P.buf("h2d", NSEQ * NTT)
    B_out = P.buf("out")

    def I(eng, meth, reads, writes, *a, **kw):
        P.op(eng, lambda e: getattr(e, meth)(*a, **kw), reads, writes)

    def DMA(eng, out, in_, reads, writes, **kw):
        P.op(eng, lambda e: e.dma_start(out=out, in_=in_, **kw), reads, writes, dma=1)

    def DMAs(eng, pairs, reads, writes):
        P.op(eng, lambda e: [e.dma_start(out=o, in_=i) for (o, i) in pairs], reads, writes, dma=len(pairs))

    psall = nc.alloc_psum_tensor("psall", [128, 4096], F32)
    psb = [psall[:, i * 512:(i + 1) * 512] for i in range(8)]
    B_ps = P.buf("ps", 8)

    es_glob = ExitStack()

    _cnt = [0]

    def sb(es, name, shape, dt):
        _cnt[0] += 1
        return es.enter_context(nc.sbuf_tensor(f"sb{_cnt[0]}_{name}", list(shape), dt))

    ones = sb(es_glob, "ones", [128, 128], BF16)
    onesA = sb(es_glob, "onesA", [128, 128], BF16)
    onesB = sb(es_glob, "onesB", [128, 128], BF16)
    cact = sb(es_glob, "cact", [128, 8, 2], F32)
    gT = sb(es_glob, "gT", [128, NL, 2, 8], F32)
    gfin = sb(es_glob, "gfin", [128, 8], F32)
    convw = sb(es_glob, "convw", [128, NL, 3, 4], F32)
    convb = sb(es_glob, "convb", [128, NL, 3], F32)
    lruv = sb(es_glob, "lruv", [128, 3, NL * 6], F32)
    clam = sb(es_glob, "clam", [128, NL * 6], F32)
    mod = sb(es_glob, "mod", [128, 48, 2], F32)
    A1 = sb(es_glob, "A1", [128, 8, 2], F32)
    A2 = sb(es_glob, "A2", [128, 8, 2], F32)
    epsb = sb(es_glob, "epsb", [128, 1], F32)
    B_const = P.buf("const")
    B_mod = P.buf("mod")

    with ExitStack() as es:
        t = [sb(es, f"sp_t{i}", [128, NL * 6], F32) for i in range(6)]
        I("pool", "memset", [], [B_const], ones[:], 1.0)
        I("pool", "memset", [], [B_const], onesA[:], 0.0)
        I("pool", "memset", [], [B_const], onesB[:], 0.0)
        I("pool", "memset", [B_const], [B_const], onesA[:, 0:64], 1.0)
        I("pool", "memset", [B_const], [B_const], onesB[:, 64:128], 1.0)
        I("pool", "memset", [], [B_const], epsb[:], EPS)
        Bs = P.buf("setup_ld")
        DMAs("sp", [(cact[:], cT_h.ap()), (gT[:], gT_h.ap()), (gfin[:], gfin_h.ap()),
                    (convw[:], convw_h.ap()), (convb[:], convb_h.ap()),
                    (lruv[:], lruv_h.ap().rearrange("p a l d c -> p a (l d c)"))], [], [Bs])
        Bc = P.buf("setup_c")
        I("act", "activation", [Bs], [Bc], out=cact[:], in_=cact[:], func=AF.Silu)
        lam = lruv[:, 2, :]
        Bt = P.buf("setup_t")
        V = lambda *a, **k: I("dve", *a, **k)
        rw = ([Bs, Bt], [Bt])
        V("tensor_scalar", *rw, out=t[5][:], in0=lam, scalar1=-1.0, scalar2=None, op0=ALU.mult)
        V("tensor_tensor", *rw, out=t[0][:], in0=lam, in1=t[5][:], op=ALU.max)
        I("act", "activation", [Bt], [Bt], out=t[1][:], in_=t[0][:], func=AF.Exp, scale=-1.0)
        V("tensor_scalar", *rw, out=t[2][:], in0=t[1][:], scalar1=2.0, scalar2=None, op0=ALU.add)
        V("reciprocal", *rw, out=t[2][:], in_=t[2][:])
        V("tensor_tensor", *rw, out=t[2][:], in0=t[1][:], in1=t[2][:], op=ALU.mult)
        V("tensor_tensor", *rw, out=t[3][:], in0=t[2][:], in1=t[2][:], op=ALU.mult)
        V("tensor_scalar", *rw, out=t[4][:], in0=t[3][:], scalar1=1.0 / 13, scalar2=1.0 / 11, op0=ALU.mult, op1=ALU.add)
        for cf in (1.0 / 9, 1.0 / 7, 1.0 / 5, 1.0 / 3, 1.0):
            V("tensor_tensor", *rw, out=t[4][:], in0=t[4][:], in1=t[3][:], op=ALU.mult)
            V("tensor_scalar", *rw, out=t[4][:], in0=t[4][:], scalar1=cf, scalar2=None, op0=ALU.add)
        V("tensor_tensor", *rw, out=t[4][:], in0=t[4][:], in1=t[2][:], op=ALU.mult)
        V("tensor_scalar", *rw, out=t[5][:], in0=lam, scalar1=-1.0, scalar2=0.0, op0=ALU.mult, op1=ALU.max)
        V("scalar_tensor_tensor", *rw, out=t[5][:], in0=t[4][:], scalar=2.0, in1=t[5][:], op0=ALU.mult, op1=ALU.add)
        V("tensor_scalar", [Bt], [B_const], out=clam[:], in0=t[5][:], scalar1=-8.0, scalar2=None, op0=ALU.mult)
        P.flush()

    def norm_tile(xt, Bx, sq, Bsq, rs, Brs, ssb):
        I("act", "activation", [Bx], [Bsq], out=sq[:], in_=xt[:], func=AF.Square)
        for c in range(8):
            I("pe", "matmul", [Bsq, B_const], B_ps[ssb], psb[ssb][:], ones[:], sq[:, c, :],
              start=(c == 0), stop=(c == 7))
        I("act", "activation", B_ps[ssb] + [B_const], [Brs], out=rs[:], in_=psb[ssb][:], func=AF.Ln,
          scale=1.0 / D, bias=epsb[:])
        I("act", "activation", [Brs], [Brs], out=rs[:], in_=rs[:], func=AF.Exp, scale=-0.5)

    for l in range(NL):
        es_layer = ExitStack()
        w_in_sb = sb(es_layer, f"w_in_sb{l}", [128, 8, D_IN], BF16)
        EI = sb(es_layer, f"EI{l}", [128, 6, 5, 2, 64], BF16)
        EB = sb(es_layer, f"EB{l}", [128, 4, 6, 4, 2, 64], BF16)
        WG = sb(es_layer, f"WG{l}", [128, 2, 2, 3, 128], BF16)
        ABall = sb(es_layer, f"ABall{l}", [128, 2, 2, 64], BF16)
        B_win = P.buf(f"win{l}", 8)
        B_tab = P.buf(f"tab{l}", 96)
        B_wg = P.buf(f"wg{l}", 8)
        B_ab = P.buf(f"ab{l}")

        with ExitStack() as es:
            stg = [sb(es, f"ada_stg{i}", [128, 6 * D], BF16) for i in range(2)]
            cact_bf = sb(es, "cact_bf", [128, 8, 2], BF16)
            B_cbf = P.buf("cact_bf")
            I("dve", "tensor_copy", [Bc], [B_cbf], out=cact_bf[:], in_=cact[:])
            B_stg = P.buf("ada_stg", 2)
            badaT = sb(es, "badaT", [128, 48], F32)
            B_bada = P.buf("bada")
            DMA("sp", badaT[:], b_adaT_h.ap()[l], [], [B_bada])
            for k in range(8):
                DMAs("pool", [(w_in_sb[:, k, 0:1088], w_in[l, k * 128:(k + 1) * 128, 0:1088]),
                              (w_in_sb[:, k, 1088:2176], w_in[l, k * 128:(k + 1) * 128, 1088:2176])],
                     [], B_win[k])
            for k in range(8):
                DMAs("pool", [(stg[k % 2][:, h * 1536:(h + 1) * 1536], w_ada[l, k * 128:(k + 1) * 128, h * 1536:(h + 1) * 1536])
                              for h in range(4)], [], B_stg[k % 2])
                for j in range(48):
                    I("pe", "matmul", B_stg[k % 2] + [B_cbf], B_ps[0], psb[0][:, 2 * j:2 * j + 2],
                      stg[k % 2][:, j * 128:(j + 1) * 128], cact_bf[:, k, :],
                      start=(k == 0 and j == 0), stop=(k == 7 and j == 47), skip_group_check=True)
            for s in range(2):
                I("dve", "tensor_tensor", B_ps[0] + [B_bada], [B_mod], out=mod[:, :, s],
                  in0=psb[0][:, 0:96].rearrange("p (j s) -> p j s", s=2)[:, :, s], in1=badaT[:], op=ALU.add)
                I("dve", "scalar_tensor_tensor", [B_mod, Bs], [B_mod], out=A1[:, :, s], in0=mod[:, 8:16, s],
                  scalar=1.0, in1=gT[:, l, 0, :], op0=ALU.add, op1=ALU.mult)
                I("dve", "scalar_tensor_tensor", [B_mod, Bs], [B_mod], out=A2[:, :, s], in0=mod[:, 32:40, s],
                  scalar=1.0, in1=gT[:, l, 1, :], op0=ALU.add, op1=ALU.mult)
            I("pool", "memset", [], [B_wg], WG[:], 0.0)
            for dr_ in range(2):
                for ax, wh in enumerate((lru_wa_h, lru_wx_h)):
                    for hh in range(2):
                        src = wh.ap()[l, dr_, hh::2, :, :].rearrange("h d e -> d h e")
                        DMA("pool", WG[hh * 64:(hh + 1) * 64, dr_, ax, :, hh * 64:(hh + 1) * 64], src, [], B_wg[(dr_ * 2 + ax) * 2 + hh])
            cd_sb = sb(es, "cd_sb", [64, 2, 64], F32)
            wf_sb = sb(es, "wf_sb", [64, 4, 64], F32)
            B_f = P.buf("four_ld")
            DMAs("sp", [(cd_sb[:], cd_h.ap()), (wf_sb[:], w_four_h.ap()[l].rearrange("g d e -> d g e"))], [], [B_f])
            for g in range(4):
                hsl = slice((g % 2) * 64, (g % 2) * 64 + 64)
                for ri in range(2):
                    c0 = ((g // 2) * 2 + ri) * 64
                    I("pe", "matmul", [B_f], B_ps[1], psb[1][hsl, c0:c0 + 64],
                      cd_sb[:, ri, :], wf_sb[:, g, :], start=True, stop=True)
            I("dve", "tensor_copy", B_ps[1], [B_ab], out=ABall[:].rearrange("p q r e -> p (q r e)"), in_=psb[1][:, 0:256])
            Efull = sb(es, "Efull", [128, 90, 64], F32)
            Ebf = sb(es, "Ebf", [128, 6, 15, 64], BF16)
            cm_sb = sb(es, "cm_sb", [128, 64], F32)
            B_e = P.buf("efull")
            src = rpb_h.ap()[l].rearrange("r k q -> k r q")
            DMAs("sp", [(Efull[0:64], src), (Efull[64:128], src), (cm_sb[:], cm_h.ap())], [], [B_e])
            I("act", "activation", [B_e], [B_e], out=Efull[:], in_=Efull[:], func=AF.Exp)
            cmb = bass.AP(tensor=cm_sb, offset=0, ap=[[64, 128], [0, 90], [1, 64]])
            I("dve", "tensor_tensor", [B_e], [B_e], out=Ebf[:].rearrange("p h r q -> p (h r) q"), in0=Efull[:], in1=cmb, op=ALU.mult)
            n = 0
            for a in range(2):
                pa = slice(a * 64, (a + 1) * 64)
                for b in range(2):
                    for jj in range(5):
                        dr = 2 * (jj - 2) + a - b
                        eng = "dve"
                        n += 1
                        if -4 <= dr <= 3:
                            I(eng, "tensor_copy", [B_e], B_tab[n], out=EI[pa, :, jj, b, :], in_=Ebf[pa, :, dr + 7, :])
                        else:
                            I(eng, "memset", [], B_tab[n], EI[pa, :, jj, b, :], 0.0)
                    for wm, off in enumerate((0, 2, 4, 6)):
                        for j in range(4):
                            dr = 2 * j + a - off - b
                            eng = "dve"
                            n += 1
                            I(eng, "tensor_copy", [B_e], B_tab[n], out=EB[pa, wm, :, j, b, :], in_=Ebf[pa, :, dr + 7, :])
            P.flush()

        for s in range(NSEQ):
            x_src = xT if l == 0 else xs
            with ExitStack() as es:
                xt = [sb(es, f"a_xt{i}", [128, 8, TT], F32) for i in range(2)]
                sq = sb(es, "a_sq", [128, 8, TT], BF16)
                rs = [sb(es, f"a_rs{i}", [128, TT], F32) for i in range(2)]
                tmp = [sb(es, f"a_tmp{i}", [128, TT], F32) for i in range(2)]
                hT = [sb(es, f"a_hT{i}", [128, 8, TT], BF16) for i in range(2)]
                sgb = [sb(es, f"a_sgb{i}", [128, 11, TT], BF16) for i in range(2)]
                sgx = [sb(es, f"a_sgx{i}", [128, 3, TT], F32) for i in range(2)]
                sgv = [sb(es, f"a_sgv{i}", [128, 4, 384], BF16) for i in range(2)]
                B_xt = P.buf("a_xt", 2)
                B_sq = P.buf("a_sq")
                B_rs = P.buf("a_rs", 2)
                B_tmp = P.buf("a_tmp", 2)
                B_hT = P.buf("a_hT", 16)
                B_sgb = P.buf("a_sgb", 22)
                B_sgx = P.buf("a_sgx", 6)
                B_sgv = P.buf("a_sgv", 8)
                B_pst = P.buf("a_pst", NTT * 3)
                pbank = 0
                ntmp = 0
                nev = 0
                def a_load(tt):
                    pp = tt % 2
                    tsl = slice(tt * TT, (tt + 1) * TT)
                    DMAs("sp", [(xt[pp][:, 0:4, :], x_src[s, 0:4, :, tsl].rearrange("c p t -> p c t")),
                                (xt[pp][:, 4:8, :], x_src[s, 4:8, :, tsl].rearrange("c p t -> p c t"))],
                         B_xs[s * NTT + tt], B_xt[pp])

                def a_norm_steps(tt):
                    pp = tt % 2
                    steps = []
                    steps.append(lambda: I("act", "activation", B_xt[pp], [B_sq], out=sq[:], in_=xt[pp][:], func=AF.Square))

                    def ssmm():
                        for c in range(8):
                            I("pe", "matmul", [B_sq, B_const], B_ps[7], psb[7], ones[:], sq[:, c, :], start=(c == 0), stop=(c == 7))
                    steps.append(ssmm)

                    def lnexp():
                        I("act", "activation", B_ps[7] + [B_const], B_rs[pp], out=rs[pp][:], in_=psb[7], func=AF.Ln, scale=1.0 / D, bias=epsb[:])
                        I("act", "activation", B_rs[pp], B_rs[pp], out=rs[pp][:], in_=rs[pp][:], func=AF.Exp, scale=-0.5)
                    steps.append(lnexp)
                    for c in range(8):
                        def hstep(c=c):
                            q = ntmp_[0] % 2
                            ntmp_[0] += 1
                            I("dve", "tensor_tensor", B_xt[pp] + B_rs[pp], B_tmp[q], out=tmp[q][:], in0=xt[pp][:, c, :], in1=rs[pp][:], op=ALU.mult)
                            I("act", "activation", B_tmp[q] + [B_mod], B_hT[pp * 8 + c], out=hT[pp][:, c, :], in_=tmp[q][:],
                              func=AF.Identity, scale=A1[:, c, s:s + 1], bias=mod[:, c, s:s + 1])
                        steps.append(hstep)
                    return steps

                ntmp_ = [0]
                a_load(0)
                a_load(1)
                for st in a_norm_steps(0):
                    st()
                for tt in range(NTT):
                    pp = tt % 2
                    tsl = slice(tt * TT, (tt + 1) * TT)
                    nsteps = a_norm_steps(tt + 1) if tt + 1 < NTT else []
                    sched = {0: [0], 2: [1], 3: [2]}
                    for c in range(8):
                        sched.setdefault(4 + c, []).append(3 + c)
                    hrd = B_hT[pp * 8:(pp + 1) * 8]
                    for ci in range(18):
                        if nsteps:
                            for si in sched.get(ci, []):
                                nsteps[si]()
                        bk = pbank
                        pbank = (pbank + 1) % 6
                        if ci < 14:
                            col0 = ci * 128
                            for k in range(8):
                                I("pe", "matmul", hrd + [B_win], B_ps[bk], psb[bk], w_in_sb[:, k, col0:col0 + 128], hT[pp][:, k, :],
                                  start=(k == 0), stop=(k == 7))
                            if ci < 2:
                                dst, Bd, kind = sgb[pp][:, ci, :], B_sgb[pp * 11 + ci], "copy"
                            elif ci < 5:
                                dst, Bd, kind = sgx[pp][:, ci - 2, :], B_sgx[pp * 3 + ci - 2], "copy"
                            elif ci < 8:
                                dst, Bd, kind = sgb[pp][:, 2 + ci - 5, :], B_sgb[pp * 11 + 2 + ci - 5], "gelu"
                            else:
                                dst, Bd, kind = sgb[pp][:, 5 + ci - 8, :], B_sgb[pp * 11 + 5 + ci - 8], "copy"
                            if kind == "gelu":
                                I("act", "activation", B_ps[bk], Bd, out=dst, in_=psb[bk], func=AF.Gelu_apprx_tanh)
                            else:
                                nev += 1
                                if nev % 3 == 0:
                                    I("act", "activation", B_ps[bk], Bd, out=dst, in_=psb[bk], func=AF.Identity)
                                else:
                                    I("dve", "tensor_copy", B_ps[bk], Bd, out=dst, in_=psb[bk])
                        else:
                            sub = ci - 14
                            for k in range(8):
                                I("pe", "matmul", hrd + [B_win], B_ps[bk], psb[bk][:, 0:384], hT[pp][:, k, sub * 128:(sub + 1) * 128],
                                  w_in_sb[:, k, 1792:2176], start=(k == 0), stop=(k == 7))
                            I("dve", "tensor_copy", B_ps[bk], B_sgv[pp * 4 + sub], out=sgv[pp][:, sub, :], in_=psb[bk][:, 0:384])
                    if tt + 2 < NTT:
                        a_load(tt + 2)
                    DMAs("sp", [(pU[s, :, :, tsl].rearrange("c p t -> p c t"), sgb[pp][:, 0:2, :]),
                                (pgl[s, :, :, tsl].rearrange("c p t -> p c t"), sgb[pp][:, 2:5, :]),
                                (pq[s, :, :, tsl].rearrange("c p t -> p c t"), sgb[pp][:, 5:8, :]),
                                (pk[s, :, :, tsl].rearrange("c p t -> p c t"), sgb[pp][:, 8:11, :])],
                         B_sgb[pp * 11:(pp + 1) * 11], B_pst[tt * 3])
                    DMA("sp", ppx[s, :, :, tsl].rearrange("c p t -> p c t"), sgx[pp][:], B_sgx[pp * 3:(pp + 1) * 3], B_pst[tt * 3 + 1])
                    DMA("sp", pv[s, tt * 4:(tt + 1) * 4, :, :].rearrange("t p c -> p t c"), sgv[pp][:], B_sgv[pp * 4:(pp + 1) * 4], B_pst[tt * 3 + 2])
                P.flush()

            with ExitStack() as es:
                Mc = sb(es, "f_Mc", [128, 64, 2, 64], BF16)
                F01 = sb(es, "f_F01", [128, 2, 128], BF16)
                UT = [sb(es, f"f_UT{i}", [128, S], BF16) for i in range(2)]
                XsL = [sb(es, f"f_Xs{i}", [128, 64, 2, 64], BF16) for i in range(2)]
                PsL = [sb(es, f"f_Ps{i}", [128, 2, 64, 64], BF16) for i in range(2)]
                yf = [sb(es, f"f_yf{i}", [128, S], BF16) for i in range(2)]
                B_fc = P.buf("f_c")
                B_UT = P.buf("f_UT", 2)
                B_XsL = [P.buf(f"f_Xs{i}", 32) for i in range(2)]
                B_PsL = [P.buf(f"f_Ps{i}", 32) for i in range(2)]
                B_yf = P.buf("f_yf", 32)
                McF = Mc[:].rearrange("p a b c -> p (a b c)")
                DMAs("pool", [(McF[h * 64:(h + 1) * 64, i * 2048:(i + 1) * 2048], Mc_h.ap()[:, i * 2048:(i + 1) * 2048])
                              for h in range(2) for i in range(4)]
                     + [(F01[h * 64:(h + 1) * 64], F_h.ap()) for h in range(2)], [], [B_fc])
                fst = {"nev": 0, "pbank": 0}
                HS = (slice(0, 64), slice(64, 128))

                def f_evac(bk, Bd, dst, src):
                    fst["nev"] += 1
                    if fst["nev"] % 2 == 0:
                        I("act", "activation", B_ps[bk], Bd, out=dst, in_=src, func=AF.Identity)
                    else:
                        I("dve", "tensor_copy", B_ps[bk], Bd, out=dst, in_=src)

                def f_banks():
                    bk = fst["pbank"]
                    fst["pbank"] = (bk + 2) % 8
                    return (bk, bk + 1)

                def f_s0(q):
                    Xs, B_Xs = XsL[q], B_XsL[q]
                    DMA("sp", UT[q][:], pU[s, q], B_p[s], B_UT[q])
                    for s2b in range(16):
                        bks = f_banks()
                        for i in range(4):
                            s2 = s2b * 4 + i
                            for h in range(2):
                                I("pe", "matmul", B_UT[q] + [B_ab], B_ps[bks[h]], psb[bks[h]][HS[h], i * 128:(i + 1) * 128],
                                  UT[q][HS[h], s2::64], ABall[HS[h], q, :, :].rearrange("p r e -> p (r e)"), start=True, stop=True)
                        for h in range(2):
                            f_evac(bks[h], B_Xs[s2b * 2 + h], Xs[HS[h], s2b * 4:(s2b + 1) * 4, :, :].rearrange("p a r e -> p (a r e)"),
                                   psb[bks[h]][HS[h], :])

                def f_s1(q):
                    Xs, B_Xs, Ps, B_Ps = XsL[q], B_XsL[q], PsL[q], B_PsL[q]
                    for eb in range(16):
                        bks = f_banks()
                        for i in range(4):
                            e_ = eb * 4 + i
                            for ri in range(2):
                                for h in range(2):
                                    I("pe", "matmul", [B_Xs, B_fc], B_ps[bks[h]], psb[bks[h]][HS[h], i * 128:(i + 1) * 128],
                                      Xs[HS[h], :, ri, e_], F01[HS[h], ri, :], start=(ri == 0), stop=(ri == 1))
                        for h in range(2):
                            f_evac(bks[h], B_Ps[eb * 2 + h], Ps[HS[h], :, :, eb * 4:(eb + 1) * 4].rearrange("p r k e -> p (r k) e"),
                                   psb[bks[h]][HS[h], :].rearrange("p (e r) -> p r e", e=4))

                def f_s3(q):
                    Ps, B_Ps = PsL[q], B_PsL[q]
                    for kb in range(8):
                        bks = f_banks()
                        for i in range(8):
                            k1 = kb * 8 + i
                            for ri in range(2):
                                for h in range(2):
                                    I("pe", "matmul", [B_Ps, B_fc], B_ps[bks[h]], psb[bks[h]][HS[h], i * 64:(i + 1) * 64],
                                      Ps[HS[h], ri, k1, :], Mc[HS[h], k1, ri, :], start=(ri == 0), stop=(ri == 1))
                        for h in range(2):
                            f_evac(bks[h], B_yf[q * 16 + kb * 2 + h], yf[q][HS[h], :].rearrange("p (k2 k1) -> p k1 k2", k1=64)[:, kb * 8:(kb + 1) * 8, :],
                                   psb[bks[h]][HS[h], :].rearrange("p (a b) -> p a b", a=8))
                    DMA("sp", yTd[s, q], yf[q][:], B_yf[q * 16:(q + 1) * 16], B_y[s])

                f_s0(0)
                f_s0(1)
                f_s1(0)
                f_s1(1)
                f_s3(0)
                f_s3(1)
                P.flush()

            with ExitStack() as es:
                W = [sb(es, f"l_W{i}", [128, S], F32) for i in range(7)]
                ubf = sb(es, "l_ubf", [128, S], BF16)
                gl = sb(es, "l_gl", [128, S], BF16)
                yl = sb(es, "l_yl", [128, S], BF16)
                B_W = [P.buf(f"l_W{i}", 8) for i in range(7)]
                B_ubf = P.buf("l_ubf")
                B_gl = P.buf("l_gl")
                B_yl = P.buf("l_yl")
                lst = {"pbank": 0}
                roles = [list(range(7))]
                for c in range(1, 3):
                    r = roles[-1]
                    roles.append([r[2], r[3], r[1], r[4], r[5], r[6], r[0]])

                def RW(c, k):
                    return W[roles[c][k]], B_W[roles[c][k]]

                def l_front(c):
                    (px, Bpx), (u, Bu) = RW(c, 0), RW(c, 1)
                    DMAs("sp", [(px[:, 0:2048], ppx[s, c, :, 0:2048]), (px[:, 2048:4096], ppx[s, c, :, 2048:4096])], B_p[s], [Bpx])
                    cw = lambda k: convw[:, l, c, k:k + 1]
                    I("act", "activation", [Bpx, Bs], [Bu], out=u[:], in_=px[:], func=AF.Identity, scale=cw(2), bias=convb[:, l, c:c + 1])
                    I("dve", "scalar_tensor_tensor", [Bpx, Bu, Bs], [Bu], out=u[:, 2:S], in0=px[:, 0:S - 2], scalar=cw(0), in1=u[:, 2:S], op0=ALU.mult, op1=ALU.add)
                    I("dve", "scalar_tensor_tensor", [Bpx, Bu, Bs], [Bu], out=u[:, 1:S], in0=px[:, 0:S - 1], scalar=cw(1), in1=u[:, 1:S], op0=ALU.mult, op1=ALU.add)
                    I("dve", "scalar_tensor_tensor", [Bpx, Bu, Bs], [Bu], out=u[:, 0:S - 1], in0=px[:, 1:S], scalar=cw(3), in1=u[:, 0:S - 1], op0=ALU.mult, op1=ALU.add)

                def l_cast(c):
                    (u, Bu) = RW(c, 1)
                    I("act", "activation", [Bu], [B_ubf], out=ubf[:], in_=u[:], func=AF.Identity)

                def l_sig(c, dr_):
                    (ra, Bra), (ib, Bib) = RW(c, 2 + 2 * dr_), RW(c, 3 + 2 * dr_)
                    (u, Bu) = RW(c, 1)
                    vi = (l * 2 + dr_) * 3 + c
                    for tt in range(NTT):
                        tsl = slice(tt * TT, (tt + 1) * TT)
                        for ax, (dstw, Bd) in enumerate(((ra, Bra), (ib, Bib))):
                            bk = lst["pbank"]
                            lst["pbank"] = (bk + 1) % 8
                            I("pe", "matmul", [B_ubf, B_wg], B_ps[bk], psb[bk], WG[:, dr_, ax, c, :], ubf[:, tsl], start=True, stop=True)
                            I("act", "activation", B_ps[bk] + [Bs], Bd[tt], out=dstw[:, tsl], in_=psb[bk], func=AF.Sigmoid,
                              bias=lruv[:, ax, vi:vi + 1])
                    I("dve", "tensor_tensor", [Bib, Bu], [Bib], out=ib[:], in0=ib[:], in1=u[:], op=ALU.mult)

                def l_rest(c, dr_):
                    (ra, Bra), (ib, Bib) = RW(c, 2 + 2 * dr_), RW(c, 3 + 2 * dr_)
                    (tmpb, Btmp) = RW(c, 0 if dr_ == 0 else 6)
                    vi = (l * 2 + dr_) * 3 + c
                    I("act", "activation", [Bra, B_const], [Bra], out=ra[:], in_=ra[:], func=AF.Exp, scale=clam[:, vi:vi + 1])
                    I("act", "activation", [Bra], [Btmp], out=tmpb[:], in_=ra[:], func=AF.Square)
                    I("act", "activation", [Btmp], [Btmp], out=tmpb[:], in_=tmpb[:], func=AF.Sqrt, scale=-1.0, bias=1.0)
                    I("pool", "tensor_tensor", [Bib, Btmp], [Bib], out=ib[:], in0=ib[:], in1=tmpb[:], op=ALU.mult)
                    if dr_ == 0:
                        I("dve", "tensor_tensor_scan", [Bra, Bib], [Btmp], out=tmpb[:], data0=ra[:], data1=ib[:], initial=0.0,
                          op0=ALU.mult, op1=ALU.add)
                    else:
                        I("dve", "tensor_tensor_scan", [Bra, Bib], [Btmp], out=tmpb[:, ::-1], data0=ra[:, ::-1], data1=ib[:, ::-1],
                          initial=0.0, op0=ALU.mult, op1=ALU.add)

                def l_final(c):
                    (h0, Bh0), (h1, Bh1) = RW(c, 0), RW(c, 6)
                    DMA("sp", gl[:], pgl[s, c], B_p[s], [B_gl])
                    I("dve", "tensor_tensor", [Bh0, Bh1], [Bh0], out=h0[:], in0=h0[:], in1=h1[:], op=ALU.add)
                    I("dve", "tensor_tensor", [Bh0, B_gl], [B_yl], out=yl[:], in0=h0[:], in1=gl[:], op=ALU.mult)
                    DMA("sp", yTd[s, 2 + c], yl[:], [B_yl], B_y[s])

                l_front(0)
                l_cast(0)
                for c in range(3):
                    l_sig(c, 0)
                    l_rest(c, 0)
                    l_sig(c, 1)
                    if c + 1 < 3:
                        l_front(c + 1)
                    l_rest(c, 1)
                    if c + 1 < 3:
                        l_cast(c + 1)
                    l_final(c)
                P.flush()

            with ExitStack() as es:
                qTd = [sb(es, f"n_q{i}", [128, S], BF16) for i in range(2)]
                kzd = [[sb(es, f"n_kz{i}_{h}", [128, S], BF16) for h in range(2)] for i in range(2)]
                vpd = [sb(es, f"n_vp{i}", [128, 32, 128], BF16) for i in range(2)]
                ynd = [sb(es, f"n_yn{i}", [128, S], BF16) for i in range(2)]
                pe_ = [sb(es, f"n_pe{i}", [128, 2, 5, 128], BF16) for i in range(2)]
                pt_ = [sb(es, f"n_pt{i}", [128, 2, 5, 128], BF16) for i in range(3)]
                lnd = [sb(es, f"n_lnd{i}", [128, 128], F32) for i in range(2)]
                B_qd = P.buf("n_q", 2)
                B_kd = P.buf("n_k", 2)
                B_vd = P.buf("n_v", 2)
                B_ynd = P.buf("n_yn", 64)
                B_pe = P.buf("n_pe", 6)
                B_pt = P.buf("n_pt", 3)
                B_lnd = P.buf("n_lnd", 2)
                for i in range(2):
                    for h in range(2):
                        I("pool", "memset", [], B_kd[i], kzd[i][h][:], 0.0)
                nb = 0

                def n_loads(hp_):
                    d_ = hp_ % 2
                    DMA("sp", qTd[d_][:], pq[s, hp_], B_p[s], B_qd[d_])
                    DMAs("sp", [(kzd[d_][0][0:64, :], pk[s, hp_, 0:64, :]), (kzd[d_][1][64:128, :], pk[s, hp_, 64:128, :])],
                         B_p[s] + B_kd[d_], B_kd[d_])
                    DMAs("sp", [(vpd[d_][:, half * 16:(half + 1) * 16, :],
                                 pv[s, half * 16:(half + 1) * 16, :, hp_ * 128:(hp_ + 1) * 128].rearrange("t p c -> p t c"))
                                for half in range(2)], B_p[s], B_vd[d_])

                n_loads(0)
                for hp in range(3):
                    d = hp % 2
                    qT, kz, vp, yn = qTd[d], kzd[d], vpd[d], ynd[d]
                    B_q, B_k, B_v = B_qd[d], B_kd[d], B_vd[d]
                    if hp + 1 < 3:
                        n_loads(hp + 1)

                    def blk_info(m):
                        if m < 2:
                            return list(range(4)), EB[:, m, 2 * hp:2 * hp + 2].rearrange("p h j b q -> p h (j b q)")
                        if m >= 30:
                            return list(range(28, 32)), EB[:, m - 28, 2 * hp:2 * hp + 2].rearrange("p h j b q -> p h (j b q)")
                        return list(range(m - 2, m + 3)), EI[:, 2 * hp:2 * hp + 2].rearrange("p h j b q -> p h (j b q)")

                    def scores(m, par):
                        js, tab = blk_info(m)
                        nj = len(js)
                        b0 = 3 * par
                        qs = slice(m * 128, (m + 1) * 128)
                        for hh in range(2):
                            hsl = slice(hh * 64, (hh + 1) * 64)
                            for jj, j in enumerate(js[:4]):
                                I("pe", "matmul", B_q + B_k, B_ps[b0 + hh], psb[b0 + hh][:, jj * 128:(jj + 1) * 128],
                                  kz[hh][:, j * 128:(j + 1) * 128], qT[:, qs], start=True, stop=True)
                            if nj == 5:
                                j = js[4]
                                I("pe", "matmul", B_q + B_k, B_ps[b0 + 2], psb[b0 + 2][:, hh * 128:(hh + 1) * 128],
                                  kz[hh][:, j * 128:(j + 1) * 128], qT[:, qs], start=True, stop=True)
                        for hh in range(2):
                            I("act", "activation", B_ps[b0 + hh], B_pe[par * 3 + hh], out=pe_[par][:, hh, 0:4, :].rearrange("p j q -> p (j q)"),
                              in_=psb[b0 + hh][:], func=AF.Exp, scale=0.125)
                        if nj == 5:
                            I("act", "activation", B_ps[b0 + 2], B_pe[par * 3 + 2], out=pe_[par][:, :, 4, :],
                              in_=psb[b0 + 2][:, 0:256].rearrange("p (h q) -> p h q", h=2), func=AF.Exp, scale=0.125)
                        nq = nj * 128
                        I("dve", "tensor_tensor", B_pe[par * 3:(par + 1) * 3] + [B_tab], B_pt[m % 3],
                          out=pt_[m % 3][:].rearrange("p h j q -> p h (j q)")[:, :, 0:nq],
                          in0=pe_[par][:].rearrange("p h j q -> p h (j q)")[:, :, 0:nq], in1=tab, op=ALU.mult)

                    def pv_(m, par):
                        js, tab = blk_info(m)
                        nj = len(js)
                        tot = 2 * nj
                        n = 0
                        for hh in range(2):
                            vv = vp
                            oo = onesA if hh == 0 else onesB
                            for jj, j in enumerate(js):
                                hsl = slice(hh * 64, (hh + 1) * 64)
                                I("pe", "matmul", B_pt[m % 3] + B_v, B_ps[6], psb[6][hsl, 0:128], vv[:, j, hsl], pt_[m % 3][:, hh, jj, :],
                                  start=(jj == 0), stop=(jj == nj - 1))
                                I("pe", "matmul", B_pt[m % 3] + [B_const], B_ps[7], psb[7][hsl, 0:128], ones[:, 0:64], pt_[m % 3][:, hh, jj, :],
                                  start=(jj == 0), stop=(jj == nj - 1))
                                n += 1
                        I("act", "activation", B_ps[7], B_lnd[par], out=lnd[par][:], in_=psb[7][:, 0:128], func=AF.Ln)
                        I("act", "activation", B_lnd[par], B_lnd[par], out=lnd[par][:], in_=lnd[par][:], func=AF.Exp, scale=-1.0)
                        I("dve", "tensor_tensor", B_ps[6] + B_lnd[par], B_ynd[d * 32 + m], out=yn[:, m * 128:(m + 1) * 128], in0=psb[6][:, 0:128],
                          in1=lnd[par][:], op=ALU.mult)

                    scores(0, 0)
                    scores(1, 1)
                    for m in range(32):
                        if m + 2 < 32:
                            scores(m + 2, m % 2)
                        pv_(m, m % 2)
                        nb += 1
                    DMA("sp", yTd[s, 5 + hp], yn[:], B_ynd[d * 32:(d + 1) * 32], B_y[s])
                P.flush()

            with ExitStack() as es:
                wo = sb(es, "c_wo", [128, 8, D], BF16)
                yt = [sb(es, f"c_yt{i}", [128, 8, TT], BF16) for i in range(2)]
                xt = [sb(es, f"c_xt{i}", [128, 8, TT], F32) for i in range(3)]
                B_wo = P.buf("c_wo", 8)
                B_yt = P.buf("c_yt", 2)
                B_xt = P.buf("c_xt", 24)
                sq = sb(es, "c_sq", [128, 8, TT], BF16)
                rs = [sb(es, f"c_rs{i}", [128, TT], F32) for i in range(2)]
                tmp = [sb(es, f"c_tmp{i}", [128, TT], F32) for i in range(2)]
                h2 = [sb(es, f"c_h2{i}", [128, 8, TT], BF16) for i in range(2)]
                B_sq = P.buf("c_sq")
                B_rs = P.buf("c_rs", 2)
                B_tmp = P.buf("c_tmp", 2)
                B_h2 = P.buf("c_h2", 16)
                ntmp = 0
                for k in range(8):
                    DMAs("pool", [(wo[:, k, 0:512], w_out[l, k * 128:(k + 1) * 128, 0:512]),
                                  (wo[:, k, 512:1024], w_out[l, k * 128:(k + 1) * 128, 512:1024])], [], B_wo[k])
                pbank = 0
                def c_load(tt):
                    pp = tt % 2
                    p3 = tt % 3
                    tsl = slice(tt * TT, (tt + 1) * TT)
                    DMA("sp", yt[pp][:], yTd[s, :, :, tsl].rearrange("c p t -> p c t"), B_y[s], B_yt[pp])
                    DMAs("sp", [(xt[p3][:, 0:4, :], x_src[s, 0:4, :, tsl].rearrange("c p t -> p c t")),
                                (xt[p3][:, 4:8, :], x_src[s, 4:8, :, tsl].rearrange("c p t -> p c t"))],
                         B_xs[s * NTT + tt], B_xt[p3 * 8:(p3 + 1) * 8])

                def c_norm_steps(tt):
                    pp = tt % 2
                    p3 = tt % 3
                    tsl = slice(tt * TT, (tt + 1) * TT)
                    Bx = B_xt[p3 * 8:(p3 + 1) * 8]
                    steps = []
                    steps.append(lambda: I("act", "activation", Bx, [B_sq], out=sq[:], in_=xt[p3][:], func=AF.Square))

                    def ssmm():
                        for c in range(8):
                            I("pe", "matmul", [B_sq, B_const], B_ps[7], psb[7], ones[:], sq[:, c, :], start=(c == 0), stop=(c == 7))
                    steps.append(ssmm)

                    def lnexp():
                        I("act", "activation", B_ps[7] + [B_const], B_rs[pp], out=rs[pp][:], in_=psb[7], func=AF.Ln, scale=1.0 / D, bias=epsb[:])
                        I("act", "activation", B_rs[pp], B_rs[pp], out=rs[pp][:], in_=rs[pp][:], func=AF.Exp, scale=-0.5)
                    steps.append(lnexp)
                    for c in range(8):
                        def hstep(c=c):
                            q = ntmp_[0] % 2
                            ntmp_[0] += 1
                            I("dve", "tensor_tensor", B_xt[p3 * 8 + c] + B_rs[pp], B_tmp[q], out=tmp[q][:], in0=xt[p3][:, c, :], in1=rs[pp][:], op=ALU.mult)
                            I("act", "activation", B_tmp[q] + [B_mod], B_h2[pp * 8 + c], out=h2[pp][:, c, :], in_=tmp[q][:],
                              func=AF.Identity, scale=A2[:, c, s:s + 1], bias=mod[:, 24 + c, s:s + 1])
                        steps.append(hstep)
                    steps.append(lambda: DMA("sp", h2d[s, :, :, tsl].rearrange("c p t -> p c t"), h2[pp][:], B_h2[pp * 8:(pp + 1) * 8], B_h2d[s * NTT + tt]))
                    return steps

                ntmp_ = [0]
                c_sched = {0: [0], 2: [1], 3: [2, 3, 4], 4: [5, 6], 5: [7, 8], 6: [9, 10], 7: [11]}
                c_load(0)
                for tt in range(NTT):
                    pp = tt % 2
                    p3 = tt % 3
                    tsl = slice(tt * TT, (tt + 1) * TT)
                    if tt + 1 < NTT:
                        c_load(tt + 1)
                    nsteps = c_norm_steps(tt - 1) if tt >= 1 else []
                    for m in range(8):
                        if nsteps:
                            for si in c_sched.get(m, []):
                                nsteps[si]()
                        bk = pbank
                        pbank = (pbank + 1) % 7
                        for k in range(8):
                            I("pe", "matmul", B_yt[pp] + [B_wo], B_ps[bk], psb[bk], wo[:, k, m * 128:(m + 1) * 128], yt[pp][:, k, :],
                              start=(k == 0), stop=(k == 7))
                        I("dve", "scalar_tensor_tensor", B_ps[bk] + B_xt[p3 * 8 + m] + [B_mod], B_xt[p3 * 8 + m], out=xt[p3][:, m, :],
                          in0=psb[bk], scalar=mod[:, 16 + m, s:s + 1], in1=xt[p3][:, m, :], op0=ALU.mult, op1=ALU.add)
                    DMAs("sp", [(xs[s, 0:4, :, tsl].rearrange("c p t -> p c t"), xt[p3][:, 0:4, :]),
                                (xs[s, 4:8, :, tsl].rearrange("c p t -> p c t"), xt[p3][:, 4:8, :])],
                         B_xt[p3 * 8:(p3 + 1) * 8], B_xs[s * NTT + tt])
                for st in c_norm_steps(NTT - 1):
                    st()
                P.flush()

        es_layer.close()
        with ExitStack() as es:
            wg = sb(es, "d_wg", [128, 8, D_FF], BF16)
            wu = sb(es, "d_wu", [128, 8, D_FF], BF16)
            wd = sb(es, "d_wd", [128, NFF, D], BF16)
            xt = sb(es, "d_xt", [128, 8, TT], F32)
            h2 = [sb(es, f"d_h2{i}", [128, 8, TT], BF16) for i in range(2)]
            act = sb(es, "d_act", [128, NFF, TT], BF16)
            sg = [sb(es, f"d_sg{i}", [128, TT], BF16) for i in range(2)]
            last = (l == NL - 1)
            if last:
                fsq = sb(es, "d_fsq", [128, 8, TT], BF16)
                frs = sb(es, "d_frs", [128, TT], F32)
                B_fsq = P.buf("d_fsq")
                B_frs = P.buf("d_frs")
            B_wgu = P.buf("d_wgu", NFF)
            B_wd = P.buf("d_wd", NFF)
            B_xt = P.buf("d_xt", 8)
            B_h2 = P.buf("d_h2", 2)
            B_act = P.buf("d_act", NFF)
            B_sg = P.buf("d_sg", 2)
            for j in range(NFF):
                cs_ = slice(j * 128, (j + 1) * 128)
                DMAs("pool", [(wg[:, :, cs_], w_gate[l, :, cs_].rearrange("(k p) c -> p k c", p=128)),
                              (wu[:, :, cs_], w_up[l, :, cs_].rearrange("(k p) c -> p k c", p=128))], [], B_wgu[j])
            for j in range(NFF):
                DMAs("pool", [(wd[:, j, 0:512], w_down[l, j * 128:(j + 1) * 128, 0:512]),
                              (wd[:, j, 512:1024], w_down[l, j * 128:(j + 1) * 128, 512:1024])], [], B_wd[j])
            nsg = 0

            def d_load(n):
                s_, tt_ = divmod(n, NTT)
                DMA("sp", h2[n % 2][:], h2d[s_, :, :, tt_ * TT:(tt_ + 1) * TT].rearrange("c p t -> p c t"),
                    B_h2d[s_ * NTT + tt_], B_h2[n % 2])

            d_load(0)
            n = 0
            pending = []

            def x_loads(s_, tt_):
                tsl_ = slice(tt_ * TT, (tt_ + 1) * TT)
                DMAs("sp", [(xt[:, c, :], xs[s_, c, :, tsl_]) for c in range(4)], B_xs[s_ * NTT + tt_], B_xt[0:4])
                DMAs("sp", [(xt[:, c, :], xs[s_, c, :, tsl_]) for c in range(4, 8)], B_xs[s_ * NTT + tt_], B_xt[4:8])

            def fin_steps(s_, tt_):
                tsl_ = slice(tt_ * TT, (tt_ + 1) * TT)
                st = []
                st.append(lambda: I("act", "activation", [B_xt], [B_fsq], out=fsq[:], in_=xt[:], func=AF.Square))

                def ssmm():
                    for c in range(8):
                        I("pe", "matmul", [B_fsq, B_const], B_ps[4], psb[4], ones[:], fsq[:, c, :], start=(c == 0), stop=(c == 7))
                st.append(ssmm)

                def lnexp():
                    I("act", "activation", B_ps[4] + [B_const], [B_frs], out=frs[:], in_=psb[4], func=AF.Ln, scale=1.0 / D, bias=epsb[:])
                    I("act", "activation", [B_frs], [B_frs], out=frs[:], in_=frs[:], func=AF.Exp, scale=-0.5)
                st.append(lnexp)

                def outs():
                    for c in range(8):
                        I("dve", "scalar_tensor_tensor", B_xt[c] + [B_frs, Bs], B_xt[c], out=xt[:, c, :], in0=xt[:, c, :],
                          scalar=gfin[:, c:c + 1], in1=frs[:], op0=ALU.mult, op1=ALU.mult)
                        DMA("sp", outT[s_, c, :, tsl_], xt[:, c, :], B_xt[c], [B_out])
                st.append(outs)
                return st

            for s in range(NSEQ):
                for tt in range(NTT):
                    tsl = slice(tt * TT, (tt + 1) * TT)
                    hp_ = n % 2
                    n += 1
                    if n < NSEQ * NTT:
                        d_load(n)
                    if not pending:
                        x_loads(s, tt)
                    f_sched = {0: 0, 2: 1, 3: 2, 4: 3}
                    for j in range(NFF):
                        if pending and j in f_sched:
                            pending[f_sched[j]]()
                            if j == 4:
                                pending = []
                                x_loads(s, tt)
                        bg = j % 2
                        bu = 2 + j % 2
                        for k in range(8):
                            I("pe", "matmul", B_h2[hp_] + B_wgu[j], B_ps[bg], psb[bg][:], wg[:, k, j * 128:(j + 1) * 128], h2[hp_][:, k, :],
                              start=(k == 0), stop=(k == 7))
                        for k in range(8):
                            I("pe", "matmul", B_h2[hp_] + B_wgu[j], B_ps[bu], psb[bu][:], wu[:, k, j * 128:(j + 1) * 128], h2[hp_][:, k, :],
                              start=(k == 0), stop=(k == 7))
                        q = nsg % 2
                        nsg += 1
                        I("act", "activation", B_ps[bg], B_sg[q], out=sg[q][:], in_=psb[bg][:], func=AF.Silu)
                        I("dve", "tensor_tensor", B_ps[bu] + B_sg[q], B_act[j], out=act[:, j, :], in0=psb[bu][:], in1=sg[q][:], op=ALU.mult)
                    for m in range(8):
                        bk = 4 + m % 4
                        for j in range(NFF):
                            I("pe", "matmul", B_act[j] + B_wd[j], B_ps[bk], psb[bk][:], wd[:, j, m * 128:(m + 1) * 128], act[:, j, :],
                              start=(j == 0), stop=(j == NFF - 1))
                        I("dve", "scalar_tensor_tensor", B_ps[bk] + B_xt[m] + [B_mod], B_xt[m], out=xt[:, m, :],
                          in0=psb[bk][:], scalar=mod[:, 40 + m, s:s + 1], in1=xt[:, m, :], op0=ALU.mult, op1=ALU.add)
                        if not last:
                            DMA("sp", xs[s, m, :, tsl], xt[:, m, :], B_xt[m], B_xs[s * NTT + tt])
                    if last:
                        pending = fin_steps(s, tt)
            for st_ in pending:
                st_()
            P.flush()

    P.op("sp", None, reads=[B_out])
    P.flush()
    es_glob.close()
    build.stats = dict(P.total)
    return nc


def _prep_shared(inp, NL):
    f = lambda a: np.ascontiguousarray(np.asarray(a, dtype=np.float32))
    cd, Fm, Mc, cm = _consts()
    sh = {}
    sh["w_ada"] = f(inp["w_ada"][:NL])
    sh["b_adaT"] = f(np.asarray(inp["b_ada"])[:NL].reshape(NL, 48, 128).transpose(0, 2, 1))
    g = np.stack([np.asarray(inp["g_mix"])[:NL], np.asarray(inp["g_ffn"])[:NL]], axis=1)
    sh["gT"] = f(g.reshape(NL, 2, 8, 128).transpose(3, 0, 1, 2))
    sh["gfin"] = f(np.asarray(inp["g_final"]).reshape(8, 128).T)
    sh["w_in"] = f(inp["w_in"][:NL])
    sh["w_out"] = f(inp["w_out"][:NL])
    sh["w_gate"] = f(inp["w_ffn_gate"][:NL])
    sh["w_up"] = f(inp["w_ffn_up"][:NL])
    sh["w_down"] = f(inp["w_ffn_down"][:NL])
    sh["w_four"] = f(inp["w_fourier"][:NL])
    sh["convw"] = f(np.asarray(inp["conv_w"])[:NL].reshape(NL, 4, 3, 128).transpose(3, 0, 2, 1))
    sh["convb"] = f(np.asarray(inp["conv_b"])[:NL].reshape(NL, 3, 128).transpose(2, 0, 1))
    sh["lru_wa"] = f(inp["lru_w_a"][:NL])
    sh["lru_wx"] = f(inp["lru_w_x"][:NL])
    v = np.stack([np.asarray(inp["lru_b_a"])[:NL], np.asarray(inp["lru_b_x"])[:NL],
                  np.asarray(inp["lru_lambda"])[:NL]], axis=0)
    sh["lruv"] = f(v.reshape(3, NL, 2, 3, 128).transpose(4, 0, 1, 2, 3))
    kc = np.arange(64)[:, None]
    qc = np.arange(64)[None, :]
    idx = kc - qc + 15
    valid = (idx >= 0) & (idx <= 30)
    rp = np.asarray(inp["na_rpb"], np.float32)[:NL][..., np.clip(idx, 0, 30)]
    rp = np.where(valid, rp, np.float32(0.0))
    sh["rpbtab"] = f(rp.reshape(NL, 90, 64, 64))
    sh["c_cd"] = f(cd)
    sh["c_F"] = f(Fm)
    sh["c_Mc"] = f(Mc)
    sh["c_cm"] = f(cm)
    return sh


def _prep_core(x, c, NSEQ):
    xT = np.ascontiguousarray(np.asarray(x, np.float32).transpose(0, 2, 1).reshape(NSEQ, 8, 128, S))
    cT = np.zeros((128, 8, 2), np.float32)
    cT[:, :, :NSEQ] = np.asarray(c, np.float32).reshape(NSEQ, 8, 128).transpose(2, 1, 0)
    return {"xT": xT, "cT": cT}


def run(inp, NSEQ, NL, n_cores, stop=None, maxops=None):
    nc = build(NSEQ, NL, stop, maxops)
    sh = _prep_shared(inp, NL)
    x = np.asarray(inp["x"])
    c = np.asarray(inp["c"])
    in_maps = []
    for i in range(n_cores):
        m = dict(sh)
        m.update(_prep_core(x[i * NSEQ:(i + 1) * NSEQ], c[i * NSEQ:(i + 1) * NSEQ], NSEQ))
        in_maps.append(m)
    res = run_bass_kernel_spmd(nc, in_maps, core_ids=list(range(n_cores)))
    outs = []
    for i in range(n_cores):
        o = np.asarray(res.results[i]["outT"]).reshape(NSEQ, D, S).transpose(0, 2, 1)
        outs.append(o)
    return np.ascontiguousarray(np.concatenate(outs, axis=0).astype(np.float32))


def kernel(**inputs):
    return run(inputs, 2, DEPTH, N_CORES)
```

```python
import math
from contextlib import ExitStack
import numpy as np
import concourse.bass as bass
import concourse.mybir as mybir
from concourse.bass_utils import run_bass_kernel_spmd

F32 = mybir.dt.float32
BF16 = mybir.dt.bfloat16
AF = mybir.ActivationFunctionType
ALU = mybir.AluOpType

EPOCH = 30000
SAME_ENGINE_SYNC = True

D = 1024
S = 4096
DEPTH = 4
D_IN = 2176
D_FF = 2816
NFF = 22
TT = 512
NTT = 8
EPS = 1e-6
N_CORES = 8


class Buf:
    def __init__(self, prog, name, n=1):
        self.name = name
        self.n = n
        self.lw = [None] * n
        self.rd = [[] for _ in range(n)]
        prog.bufs.append(self)

    def __getitem__(self, idx):
        if isinstance(idx, slice):
            return [(self, i) for i in range(*idx.indices(self.n))]
        if isinstance(idx, (list, tuple)):
            return [(self, i) for i in idx]
        return [(self, idx)]

    def all(self):
        return [(self, i) for i in range(self.n)]

    def reset(self):
        self.lw = [None] * self.n
        self.rd = [[] for _ in range(self.n)]


def _flat(lst):
    out = []
    for x in lst:
        if isinstance(x, Buf):
            out.extend(x.all())
        elif isinstance(x, list):
            out.extend(_flat(x))
        else:
            out.append(x)
    return out


class Op:
    __slots__ = ("eng", "fn", "chan", "seq", "waits", "signal", "tick", "kdone",
                 "is_dma", "ndma", "idx")


class Prog:
    ENGS = ("pe", "act", "dve", "pool", "sp")

    def __init__(self, nc, n_dma_sems=6):
        self.nc = nc
        self.n_dma_sems = n_dma_sems
        self.bufs = []
        self.tick = {e: 0 for e in self.ENGS}
        self.dtick = {}
        self.sems = {}
        self.nops = 0
        self.total = {e: 0 for e in self.ENGS}
        self.nflush = 0
        self.stop = 10 ** 9
        self.maxops = 10 ** 9
        self.verbose = False
        self._reset()

    def _reset(self):
        self.ops = {e: [] for e in self.ENGS}
        self.know = {e: {} for e in self.ENGS}
        self.chan_seq = {}
        self.chan_last = {}
        self.dma_rr = {e: 0 for e in self.ENGS}
        for b in self.bufs:
            b.reset()

    def buf(self, name, n=1):
        return Buf(self, name, n)

    def op(self, eng, fn, reads=(), writes=(), dma=0):
        if self.nflush >= self.stop or self.nops >= self.maxops:
            return None
        X = Op()
        X.eng = eng
        X.fn = fn
        X.is_dma = dma > 0
        X.ndma = dma
        X.signal = X.is_dma
        X.tick = None
        X.idx = self.nops
        self.nops += 1
        reads = _flat(reads)
        writes = _flat(writes)
        if X.is_dma:
            k = self.dma_rr[eng]
            self.dma_rr[eng] = (k + 1) % self.n_dma_sems
            X.chan = ("dma", eng, k)
        else:
            X.chan = eng
        X.seq = self.chan_seq.get(X.chan, 0) + 1
        self.chan_seq[X.chan] = X.seq
        deps = {}
        for (b, i) in reads:
            w = b.lw[i]
            if w is not None:
                deps[w.idx] = (w, True)
        for (b, i) in writes:
            w = b.lw[i]
            if w is not None:
                deps[w.idx] = (w, True)
            for r in b.rd[i]:
                if r.idx not in deps:
                    deps[r.idx] = (r, False)
        if X.is_dma:
            p = self.chan_last.get(X.chan)
            if p is not None:
                deps[p.idx] = (p, True)
            self.chan_last[X.chan] = X
        K = self.know[eng]
        waits = []
        for di in sorted(deps.keys(), reverse=True):
            d, hard = deps[di]
            if (not d.is_dma) and (not X.is_dma) and d.eng == eng:
                if eng == "pe" or not hard or not SAME_ENGINE_SYNC:
                    continue
            if K.get(d.chan, 0) >= d.seq:
                continue
            d.signal = True
            waits.append(d)
            for c, s in d.kdone.items():
                if K.get(c, 0) < s:
                    K[c] = s
        X.waits = waits
        kd = dict(K)
        kd[X.chan] = X.seq
        X.kdone = kd
        for (b, i) in reads:
            b.rd[i].append(X)
        for (b, i) in writes:
            b.lw[i] = X
            b.rd[i] = []
        self.ops[eng].append(X)
        return X

    def _sem(self, key):
        s = self.sems.get(key)
        if s is None:
            s = self.nc.alloc_semaphore("s_" + "_".join(str(k) for k in key))
            self.sems[key] = s
        return s

    def flush(self):
        self.nflush += 1
        if self.nflush > self.stop:
            return
        if self.verbose:
            print("flush", self.nflush, "nops", self.nops, {e: len(v) for e, v in self.ops.items()}, flush=True)
        last_dmas = list(self.chan_last.values())
        if last_dmas:
            fb = Buf(self, "_fence")
            for X in last_dmas:
                fb.rd[0].append(X)
            self.op("sp", None, writes=[fb])
            self.bufs.remove(fb)
        for e in self.ENGS:
            for X in self.ops[e]:
                if X.is_dma:
                    c = self.dtick.get(X.chan, 0) + X.ndma
                    self.dtick[X.chan] = c
                    X.tick = c * 16
                elif X.signal:
                    self.tick[e] += 1
                    X.tick = self.tick[e]
            self.total[e] += len(self.ops[e])

        def sem_val(d):
            if d.is_dma:
                return self._sem(d.chan), d.tick
            ep = (d.tick - 1) // EPOCH
            return self._sem((d.eng, ep)), d.tick - ep * EPOCH

        def run(engobj, lst):
            for X in lst:
                for d in reversed(X.waits):
                    s, v = sem_val(d)
                    engobj.wait_ge(s, v)
                if X.fn is None:
                    continue
                if X.is_dma:
                    s = self._sem(X.chan)
                    instrs = X.fn(engobj)
                    if not isinstance(instrs, (list, tuple)):
                        instrs = [instrs]
                    assert len(instrs) == X.ndma, (len(instrs), X.ndma)
                    for ins in instrs:
                        ins.then_inc(s, 16)
                else:
                    ins = X.fn(engobj)
                    if X.signal:
                        s, _ = sem_val(X)
                        ins.then_inc(s, 1)

        ops = self.ops
        with self.nc.Block() as block:
            @block.tensor
            def _(eng):
                run(eng, ops["pe"])

            @block.scalar
            def _(eng):
                run(eng, ops["act"])

            @block.vector
            def _(eng):
                run(eng, ops["dve"])

            @block.gpsimd
            def _(eng):
                run(eng, ops["pool"])

            @block.sync
            def _(eng):
                run(eng, ops["sp"])
        self._reset()


def _consts():
    d = np.arange(64)
    ang = 2.0 * np.pi * np.outer(d, d) / 64.0
    C64 = np.cos(ang)
    S64 = np.sin(ang)
    cd = np.stack([C64 / 512.0, -S64 / 512.0], axis=1).astype(np.float32)
    F = np.zeros((64, 2, 2, 64), np.float64)
    F[:, 0, 0, :] = C64
    F[:, 0, 1, :] = -S64
    F[:, 1, 0, :] = S64
    F[:, 1, 1, :] = C64
    F = F.reshape(64, 2, 128).astype(np.float32)
    s2 = np.arange(64)[:, None, None]
    k1 = np.arange(64)[None, :, None]
    k2 = np.arange(64)[None, None, :]
    th = 2.0 * np.pi * ((s2 * (k1 + 64 * k2)) % 4096) / 4096.0
    Mc = np.stack([np.cos(th), np.sin(th)], axis=2).astype(np.float32)
    qc = np.arange(64)
    cs = np.clip(qc - 8, 0, 48)
    kc = np.arange(64)[:, None]
    cm = ((kc >= cs[None, :]) & (kc < cs[None, :] + 16)).astype(np.float32)
    cm = np.concatenate([cm, cm], axis=0)
    return cd, F, Mc.reshape(64, 64 * 2 * 64), cm


def build(NSEQ=2, NL=DEPTH, stop=None, maxops=None):
    nc = bass.Bass("TRN2", target_bir_lowering=False)
    P = Prog(nc)
    if stop is not None:
        P.stop = stop
    if maxops is not None:
        P.maxops = maxops

    def dram(name, shape, dt, kind="ExternalInput"):
        return nc.dram_tensor(name, list(shape), dt, kind=kind)

    xT_h = dram("xT", [NSEQ, 8, 128, S], F32)
    cT_h = dram("cT", [128, 8, 2], F32)
    w_ada_h = dram("w_ada", [NL, D, 6 * D], F32)
    b_adaT_h = dram("b_adaT", [NL, 128, 48], F32)
    gT_h = dram("gT", [128, NL, 2, 8], F32)
    gfin_h = dram("gfin", [128, 8], F32)
    w_in_h = dram("w_in", [NL, D, D_IN], F32)
    w_out_h = dram("w_out", [NL, D, D], F32)
    w_gate_h = dram("w_gate", [NL, D, D_FF], F32)
    w_up_h = dram("w_up", [NL, D, D_FF], F32)
    w_down_h = dram("w_down", [NL, D_FF, D], F32)
    w_four_h = dram("w_four", [NL, 4, 64, 64], F32)
    convw_h = dram("convw", [128, NL, 3, 4], F32)
    convb_h = dram("convb", [128, NL, 3], F32)
    lru_wa_h = dram("lru_wa", [NL, 2, 6, 64, 64], F32)
    lru_wx_h = dram("lru_wx", [NL, 2, 6, 64, 64], F32)
    lruv_h = dram("lruv", [128, 3, NL, 2, 3], F32)
    rpb_h = dram("rpbtab", [NL, 90, 64, 64], F32)
    cd_h = dram("c_cd", [64, 2, 64], F32)
    F_h = dram("c_F", [64, 2, 128], F32)
    Mc_h = dram("c_Mc", [64, 8192], F32)
    cm_h = dram("c_cm", [128, 64], F32)
    out_h = dram("outT", [NSEQ, 8, 128, S], F32, kind="ExternalOutput")
    xs_h = dram("xs", [NSEQ, 8, 128, S], F32, kind="Internal")
    pU_h = dram("pU", [NSEQ, 2, 128, S], BF16, kind="Internal")
    pgl_h = dram("pgl", [NSEQ, 3, 128, S], BF16, kind="Internal")
    pq_h = dram("pq", [NSEQ, 3, 128, S], BF16, kind="Internal")
    pk_h = dram("pk", [NSEQ, 3, 128, S], BF16, kind="Internal")
    ppx_h = dram("ppx", [NSEQ, 3, 128, S], F32, kind="Internal")
    pv_h = dram("pv", [NSEQ, 32, 128, 384], BF16, kind="Internal")
    yT_h = dram("yTd", [NSEQ, 8, 128, S], BF16, kind="Internal")
    h2_h = dram("h2d", [NSEQ, 8, 128, S], BF16, kind="Internal")

    xT, w_ada, w_in, w_out = xT_h.ap(), w_ada_h.ap(), w_in_h.ap(), w_out_h.ap()
    w_gate, w_up, w_down = w_gate_h.ap(), w_up_h.ap(), w_down_h.ap()
    xs, pU, pgl, pq, pk, ppx, pv, yTd = (h.ap() for h in (xs_h, pU_h, pgl_h, pq_h, pk_h, ppx_h, pv_h, yT_h))
    outT = out_h.ap()
    h2d = h2_h.ap()

    B_xs = P.buf("xs", NSEQ * NTT)
    B_p = P.buf("p", NSEQ)
    B_y = P.buf("yd", NSEQ)
    B_h2d = P.buf("h2d", NSEQ * NTT)
    B_out = P.buf("out")

    def I(eng, meth, reads, writes, *a, **kw):
        P.op(eng, lambda e: getattr(e, meth)(*a, **kw), reads, writes)

    def DMA(eng, out, in_, reads, writes, **kw):
        P.op(eng, lambda e: e.dma_start(out=out, in_=in_, **kw), reads, writes, dma=1)

    def DMAs(eng, pairs, reads, writes):
        P.op(eng, lambda e: [e.dma_start(out=o, in_=i) for (o, i) in pairs], reads, writes, dma=len(pairs))

    psall = nc.alloc_psum_tensor("psall", [128, 4096], F32)
    psb = [psall[:, i * 512:(i + 1) * 512] for i in range(8)]
    B_ps = P.buf("ps", 8)

    es_glob = ExitStack()

    _cnt = [0]

    def sb(es, name, shape, dt):
        _cnt[0] += 1
        return es.enter_context(nc.sbuf_tensor(f"sb{_cnt[0]}_{name}", list(shape), dt))

    ones = sb(es_glob, "ones", [128, 128], BF16)
    onesA = sb(es_glob, "onesA", [128, 128], BF16)
    onesB = sb(es_glob, "onesB", [128, 128], BF16)
    cact = sb(es_glob, "cact", [128, 8, 2], F32)
    gT = sb(es_glob, "gT", [128, NL, 2, 8], F32)
    gfin = sb(es_glob, "gfin", [128, 8], F32)
    convw = sb(es_glob, "convw", [128, NL, 3, 4], F32)
    convb = sb(es_glob, "convb", [128, NL, 3], F32)
    lruv = sb(es_glob, "lruv", [128, 3, NL * 6], F32)
    clam = sb(es_glob, "clam", [128, NL * 6], F32)
    mod = sb(es_glob, "mod", [128, 48, 2], F32)
    A1 = sb(es_glob, "A1", [128, 8, 2], F32)
    A2 = sb(es_glob, "A2", [128, 8, 2], F32)
    epsb = sb(es_glob, "epsb", [128, 1], F32)
    B_const = P.buf("const")
    B_mod = P.buf("mod")

    with ExitStack() as es:
        t = [sb(es, f"sp_t{i}", [128, NL * 6], F32) for i in range(6)]
        I("pool", "memset", [], [B_const], ones[:], 1.0)
        I("pool", "memset", [], [B_const], onesA[:], 0.0)
        I("pool", "memset", [], [B_const], onesB[:], 0.0)
        I("pool", "memset", [B_const], [B_const], onesA[:, 0:64], 1.0)
        I("pool", "memset", [B_const], [B_const], onesB[:, 64:128], 1.0)
        I("pool", "memset", [], [B_const], epsb[:], EPS)
        Bs = P.buf("setup_ld")
        DMAs("sp", [(cact[:], cT_h.ap()), (gT[:], gT_h.ap()), (gfin[:], gfin_h.ap()),
                    (convw[:], convw_h.ap()), (convb[:], convb_h.ap()),
                    (lruv[:], lruv_h.ap().rearrange("p a l d c -> p a (l d c)"))], [], [Bs])
        Bc = P.buf("setup_c")
        I("act", "activation", [Bs], [Bc], out=cact[:], in_=cact[:], func=AF.Silu)
        lam = lruv[:, 2, :]
        Bt = P.buf("setup_t")
        V = lambda *a, **k: I("dve", *a, **k)
        rw = ([Bs, Bt], [Bt])
        V("tensor_scalar", *rw, out=t[5][:], in0=lam, scalar1=-1.0, scalar2=None, op0=ALU.mult)
        V("tensor_tensor", *rw, out=t[0][:], in0=lam, in1=t[5][:], op=ALU.max)
        I("act", "activation", [Bt], [Bt], out=t[1][:], in_=t[0][:], func=AF.Exp, scale=-1.0)
        V("tensor_scalar", *rw, out=t[2][:], in0=t[1][:], scalar1=2.0, scalar2=None, op0=ALU.add)
        V("reciprocal", *rw, out=t[2][:], in_=t[2][:])
        V("tensor_tensor", *rw, out=t[2][:], in0=t[1][:], in1=t[2][:], op=ALU.mult)
        V("tensor_tensor", *rw, out=t[3][:], in0=t[2][:], in1=t[2][:], op=ALU.mult)
        V("tensor_scalar", *rw, out=t[4][:], in0=t[3][:], scalar1=1.0 / 13, scalar2=1.0 / 11, op0=ALU.mult, op1=ALU.add)
        for cf in (1.0 / 9, 1.0 / 7, 1.0 / 5, 1.0 / 3, 1.0):
            V("tensor_tensor", *rw, out=t[4][:], in0=t[4][:], in1=t[3][:], op=ALU.mult)
            V("tensor_scalar", *rw, out=t[4][:], in0=t[4][:], scalar1=cf, scalar2=None, op0=ALU.add)
        V("tensor_tensor", *rw, out=t[4][:], in0=t[4][:], in1=t[2][:], op=ALU.mult)
        V("tensor_scalar", *rw, out=t[5][:], in0=lam, scalar1=-1.0, scalar2=0.0, op0=ALU.mult, op1=ALU.max)
        V("scalar_tensor_tensor", *rw, out=t[5][:], in0=t[4][:], scalar=2.0, in1=t[5][:], op0=ALU.mult, op1=ALU.add)
        V("tensor_scalar", [Bt], [B_const], out=clam[:], in0=t[5][:], scalar1=-8.0, scalar2=None, op0=ALU.mult)
        P.flush()

    def norm_tile(xt, Bx, sq, Bsq, rs, Brs, ssb):
        I("act", "activation", [Bx], [Bsq], out=sq[:], in_=xt[:], func=AF.Square)
        for c in range(8):
            I("pe", "matmul", [Bsq, B_const], B_ps[ssb], psb[ssb][:], ones[:], sq[:, c, :],
              start=(c == 0), stop=(c == 7))
        I("act", "activation", B_ps[ssb] + [B_const], [Brs], out=rs[:], in_=psb[ssb][:], func=AF.Ln,
          scale=1.0 / D, bias=epsb[:])
        I("act", "activation", [Brs], [Brs], out=rs[:], in_=rs[:], func=AF.Exp, scale=-0.5)

    for l in range(NL):
        es_layer = ExitStack()
        w_in_sb = sb(es_layer, f"w_in_sb{l}", [128, 8, D_IN], BF16)
        EI = sb(es_layer, f"EI{l}", [128, 6, 5, 2, 64], BF16)
        EB = sb(es_layer, f"EB{l}", [128, 4, 6, 4, 2, 64], BF16)
        WG = sb(es_layer, f"WG{l}", [128, 2, 2, 3, 128], BF16)
        ABall = sb(es_layer, f"ABall{l}", [128, 2, 2, 64], BF16)
        B_win = P.buf(f"win{l}", 8)
        B_tab = P.buf(f"tab{l}", 96)
        B_wg = P.buf(f"wg{l}", 8)
        B_ab = P.buf(f"ab{l}")

        with ExitStack() as es:
            stg = [sb(es, f"ada_stg{i}", [128, 6 * D], BF16) for i in range(2)]
            cact_bf = sb(es, "cact_bf", [128, 8, 2], BF16)
            B_cbf = P.buf("cact_bf")
            I("dve", "tensor_copy", [Bc], [B_cbf], out=cact_bf[:], in_=cact[:])
            B_stg = P.buf("ada_stg", 2)
            badaT = sb(es, "badaT", [128, 48], F32)
            B_bada = P.buf("bada")
            DMA("sp", badaT[:], b_adaT_h.ap()[l], [], [B_bada])
            for k in range(8):
                DMAs("pool", [(w_in_sb[:, k, 0:1088], w_in[l, k * 128:(k + 1) * 128, 0:1088]),
                              (w_in_sb[:, k, 1088:2176], w_in[l, k * 128:(k + 1) * 128, 1088:2176])],
                     [], B_win[k])
            for k in range(8):
                DMAs("pool", [(stg[k % 2][:, h * 1536:(h + 1) * 1536], w_ada[l, k * 128:(k + 1) * 128, h * 1536:(h + 1) * 1536])
                              for h in range(4)], [], B_stg[k % 2])
                for j in range(48):
                    I("pe", "matmul", B_stg[k % 2] + [B_cbf], B_ps[0], psb[0][:, 2 * j:2 * j + 2],
                      stg[k % 2][:, j * 128:(j + 1) * 128], cact_bf[:, k, :],
                      start=(k == 0 and j == 0), stop=(k == 7 and j == 47), skip_group_check=True)
            for s in range(2):
                I("dve", "tensor_tensor", B_ps[0] + [B_bada], [B_mod], out=mod[:, :, s],
                  in0=psb[0][:, 0:96].rearrange("p (j s) -> p j s", s=2)[:, :, s], in1=badaT[:], op=ALU.add)
                I("dve", "scalar_tensor_tensor", [B_mod, Bs], [B_mod], out=A1[:, :, s], in0=mod[:, 8:16, s],
                  scalar=1.0, in1=gT[:, l, 0, :], op0=ALU.add, op1=ALU.mult)
                I("dve", "scalar_tensor_tensor", [B_mod, Bs], [B_mod], out=A2[:, :, s], in0=mod[:, 32:40, s],
                  scalar=1.0, in1=gT[:, l, 1, :], op0=ALU.add, op1=ALU.mult)
            I("pool", "memset", [], [B_wg], WG[:], 0.0)
            for dr_ in range(2):
                for ax, wh in enumerate((lru_wa_h, lru_wx_h)):
                    for hh in range(2):
                        src = wh.ap()[l, dr_, hh::2, :, :].rearrange("h d e -> d h e")
                        DMA("pool", WG[hh * 64:(hh + 1) * 64, dr_, ax, :, hh * 64:(hh + 1) * 64], src, [], B_wg[(dr_ * 2 + ax) * 2 + hh])
            cd_sb = sb(es, "cd_sb", [64, 2, 64], F32)
            wf_sb = sb(es, "wf_sb", [64, 4, 64], F32)
            B_f = P.buf("four_ld")
            DMAs("sp", [(cd_sb[:], cd_h.ap()), (wf_sb[:], w_four_h.ap()[l].rearrange("g d e -> d g e"))], [], [B_f])
            for g in range(4):
                hsl = slice((g % 2) * 64, (g % 2) * 64 + 64)
                for ri in range(2):
                    c0 = ((g // 2) * 2 + ri) * 64
                    I("pe", "matmul", [B_f], B_ps[1], psb[1][hsl, c0:c0 + 64],
                      cd_sb[:, ri, :], wf_sb[:, g, :], start=True, stop=True)
            I("dve", "tensor_copy", B_ps[1], [B_ab], out=ABall[:].rearrange("p q r e -> p (q r e)"), in_=psb[1][:, 0:256])
            Efull = sb(es, "Efull", [128, 90, 64], F32)
            Ebf = sb(es, "Ebf", [128, 6, 15, 64], BF16)
            cm_sb = sb(es, "cm_sb", [128, 64], F32)
            B_e = P.buf("efull")
            src = rpb_h.ap()[l].rearrange("r k q -> k r q")
            DMAs("sp", [(Efull[0:64], src), (Efull[64:128], src), (cm_sb[:], cm_h.ap())], [], [B_e])
            I("act", "activation", [B_e], [B_e], out=Efull[:], in_=Efull[:], func=AF.Exp)
            cmb = bass.AP(tensor=cm_sb, offset=0, ap=[[64, 128], [0, 90], [1, 64]])
            I("dve", "tensor_tensor", [B_e], [B_e], out=Ebf[:].rearrange("p h r q -> p (h r) q"), in0=Efull[:], in1=cmb, op=ALU.mult)
            n = 0
            for a in range(2):
                pa = slice(a * 64, (a + 1) * 64)
                for b in range(2):
                    for jj in range(5):
                        dr = 2 * (jj - 2) + a - b
                        eng = "dve"
                        n += 1
                        if -4 <= dr <= 3:
                            I(eng, "tensor_copy", [B_e], B_tab[n], out=EI[pa, :, jj, b, :], in_=Ebf[pa, :, dr + 7, :])
                        else:
                            I(eng, "memset", [], B_tab[n], EI[pa, :, jj, b, :], 0.0)
                    for wm, off in enumerate((0, 2, 4, 6)):
                        for j in range(4):
                            dr = 2 * j + a - off - b
                            eng = "dve"
                            n += 1
                            I(eng, "tensor_copy", [B_e], B_tab[n], out=EB[pa, wm, :, j, b, :], in_=Ebf[pa, :, dr + 7, :])
            P.flush()

        for s in range(NSEQ):
            x_src = xT if l == 0 else xs
            with ExitStack() as es:
                xt = [sb(es, f"a_xt{i}", [128, 8, TT], F32) for i in range(2)]
                sq = sb(es, "a_sq", [128, 8, TT], BF16)
                rs = [sb(es, f"a_rs{i}", [128, TT], F32) for i in range(2)]
                tmp = [sb(es, f"a_tmp{i}", [128, TT], F32) for i in range(2)]
                hT = [sb(es, f"a_hT{i}", [128, 8, TT], BF16) for i in range(2)]
                sgb = [sb(es, f"a_sgb{i}", [128, 11, TT], BF16) for i in range(2)]
                sgx = [sb(es, f"a_sgx{i}", [128, 3, TT], F32) for i in range(2)]
                sgv = [sb(es, f"a_sgv{i}", [128, 4, 384], BF16) for i in range(2)]
                B_xt = P.buf("a_xt", 2)
                B_sq = P.buf("a_sq")
                B_rs = P.buf("a_rs", 2)
                B_tmp = P.buf("a_tmp", 2)
                B_hT = P.buf("a_hT", 16)
                B_sgb = P.buf("a_sgb", 22)
                B_sgx = P.buf("a_sgx", 6)
                B_sgv = P.buf("a_sgv", 8)
                B_pst = P.buf("a_pst", NTT * 3)
                pbank = 0
                ntmp = 0
                nev = 0
                def a_load(tt):
                    pp = tt % 2
                    tsl = slice(tt * TT, (tt + 1) * TT)
                    DMAs("sp", [(xt[pp][:, 0:4, :], x_src[s, 0:4, :, tsl].rearrange("c p t -> p c t")),
                                (xt[pp][:, 4:8, :], x_src[s, 4:8, :, tsl].rearrange("c p t -> p c t"))],
                         B_xs[s * NTT + tt], B_xt[pp])

                def a_norm_steps(tt):
                    pp = tt % 2
                    steps = []
                    steps.append(lambda: I("act", "activation", B_xt[pp], [B_sq], out=sq[:], in_=xt[pp][:], func=AF.Square))

                    def ssmm():
                        for c in range(8):
                            I("pe", "matmul", [B_sq, B_const], B_ps[7], psb[7], ones[:], sq[:, c, :], start=(c == 0), stop=(c == 7))
                    steps.append(ssmm)

                    def lnexp():
                        I("act", "activation", B_ps[7] + [B_const], B_rs[pp], out=rs[pp][:], in_=psb[7], func=AF.Ln, scale=1.0 / D, bias=epsb[:])
                        I("act", "activation", B_rs[pp], B_rs[pp], out=rs[pp][:], in_=rs[pp][:], func=AF.Exp, scale=-0.5)
                    steps.append(lnexp)
                    for c in range(8):
                        def hstep(c=c):
                            q = ntmp_[0] % 2
                            ntmp_[0] += 1
                            I("dve", "tensor_tensor", B_xt[pp] + B_rs[pp], B_tmp[q], out=tmp[q][:], in0=xt[pp][:, c, :], in1=rs[pp][:], op=ALU.mult)
                            I("act", "activation", B_tmp[q] + [B_mod], B_hT[pp * 8 + c], out=hT[pp][:, c, :], in_=tmp[q][:],
                              func=AF.Identity, scale=A1[:, c, s:s + 1], bias=mod[:, c, s:s + 1])
                        steps.append(hstep)
                    return steps

                ntmp_ = [0]
                a_load(0)
                a_load(1)
                for st in a_norm_steps(0):
                    st()
                for tt in range(NTT):
                    pp = tt % 2
                    tsl = slice(tt * TT, (tt + 1) * TT)
                    nsteps = a_norm_steps(tt + 1) if tt + 1 < NTT else []
                    sched = {0: [0], 2: [1], 3: [2]}
                    for c in range(8):
                        sched.setdefault(4 + c, []).append(3 + c)
                    hrd = B_hT[pp * 8:(pp + 1) * 8]
                    for ci in range(18):
                        if nsteps:
                            for si in sched.get(ci, []):
                                nsteps[si]()
                        bk = pbank
                        pbank = (pbank + 1) % 6
                        if ci < 14:
                            col0 = ci * 128
                            for k in range(8):
                                I("pe", "matmul", hrd + [B_win], B_ps[bk], psb[bk], w_in_sb[:, k, col0:col0 + 128], hT[pp][:, k, :],
                                  start=(k == 0), stop=(k == 7))
                            if ci < 2:
                                dst, Bd, kind = sgb[pp][:, ci, :], B_sgb[pp * 11 + ci], "copy"
                            elif ci < 5:
                                dst, Bd, kind = sgx[pp][:, ci - 2, :], B_sgx[pp * 3 + ci - 2], "copy"
                            elif ci < 8:
                                dst, Bd, kind = sgb[pp][:, 2 + ci - 5, :], B_sgb[pp * 11 + 2 + ci - 5], "gelu"
                            else:
                                dst, Bd, kind = sgb[pp][:, 5 + ci - 8, :], B_sgb[pp * 11 + 5 + ci - 8], "copy"
                            if kind == "gelu":
                                I("act", "activation", B_ps[bk], Bd, out=dst, in_=psb[bk], func=AF.Gelu_apprx_tanh)
                            else:
                                nev += 1
                                if nev % 3 == 0:
                                    I("act", "activation", B_ps[bk], Bd, out=dst, in_=psb[bk], func=AF.Identity)
                                else:
                                    I("dve", "tensor_copy", B_ps[bk], Bd, out=dst, in_=psb[bk])
                        else:
                            sub = ci - 14
                            for k in range(8):
                                I("pe", "matmul", hrd + [B_win], B_ps[bk], psb[bk][:, 0:384], hT[pp][:, k, sub * 128:(sub + 1) * 128],
                                  w_in_sb[:, k, 1792:2176], start=(k == 0), stop=(k == 7))
                            I("dve", "tensor_copy", B_ps[bk], B_sgv[pp * 4 + sub], out=sgv[pp][:, sub, :], in_=psb[bk][:, 0:384])
                    if tt + 2 < NTT:
                        a_load(tt + 2)
                    DMAs("sp", [(pU[s, :, :, tsl].rearrange("c p t -> p c t"), sgb[pp][:, 0:2, :]),
                                (pgl[s, :, :, tsl].rearrange("c p t -> p c t"), sgb[pp][:, 2:5, :]),
                                (pq[s, :, :, tsl].rearrange("c p t -> p c t"), sgb[pp][:, 5:8, :]),
                                (pk[s, :, :, tsl].rearrange("c p t -> p c t"), sgb[pp][:, 8:11, :])],
                         B_sgb[pp * 11:(pp + 1) * 11], B_pst[tt * 3])
                    DMA("sp", ppx[s, :, :, tsl].rearrange("c p t -> p c t"), sgx[pp][:], B_sgx[pp * 3:(pp + 1) * 3], B_pst[tt * 3 + 1])
                    DMA("sp", pv[s, tt * 4:(tt + 1) * 4, :, :].rearrange("t p c -> p t c"), sgv[pp][:], B_sgv[pp * 4:(pp + 1) * 4], B_pst[tt * 3 + 2])
                P.flush()

            with ExitStack() as es:
                Mc = sb(es, "f_Mc", [128, 64, 2, 64], BF16)
                F01 = sb(es, "f_F01", [128, 2, 128], BF16)
                UT = [sb(es, f"f_UT{i}", [128, S], BF16) for i in range(2)]
                XsL = [sb(es, f"f_Xs{i}", [128, 64, 2, 64], BF16) for i in range(2)]
                PsL = [sb(es, f"f_Ps{i}", [128, 2, 64, 64], BF16) for i in range(2)]
                yf = [sb(es, f"f_yf{i}", [128, S], BF16) for i in range(2)]
                B_fc = P.buf("f_c")
                B_UT = P.buf("f_UT", 2)
                B_XsL = [P.buf(f"f_Xs{i}", 32) for i in range(2)]
                B_PsL = [P.buf(f"f_Ps{i}", 32) for i in range(2)]
                B_yf = P.buf("f_yf", 32)
                McF = Mc[:].rearrange("p a b c -> p (a b c)")
                DMAs("pool", [(McF[h * 64:(h + 1) * 64, i * 2048:(i + 1) * 2048], Mc_h.ap()[:, i * 2048:(i + 1) * 2048])
                              for h in range(2) for i in range(4)]
                     + [(F01[h * 64:(h + 1) * 64], F_h.ap()) for h in range(2)], [], [B_fc])
                fst = {"nev": 0, "pbank": 0}
                HS = (slice(0, 64), slice(64, 128))

                def f_evac(bk, Bd, dst, src):
                    fst["nev"] += 1
                    if fst["nev"] % 2 == 0:
                        I("act", "activation", B_ps[bk], Bd, out=dst, in_=src, func=AF.Identity)
                    else:
                        I("dve", "tensor_copy", B_ps[bk], Bd, out=dst, in_=src)

                def f_banks():
                    bk = fst["pbank"]
                    fst["pbank"] = (bk + 2) % 8
                    return (bk, bk + 1)

                def f_s0(q):
                    Xs, B_Xs = XsL[q], B_XsL[q]
                    DMA("sp", UT[q][:], pU[s, q], B_p[s], B_UT[q])
                    for s2b in range(16):
                        bks = f_banks()
                        for i in range(4):
                            s2 = s2b * 4 + i
                            for h in range(2):
                                I("pe", "matmul", B_UT[q] + [B_ab], B_ps[bks[h]], psb[bks[h]][HS[h], i * 128:(i + 1) * 128],
                                  UT[q][HS[h], s2::64], ABall[HS[h], q, :, :].rearrange("p r e -> p (r e)"), start=True, stop=True)
                        for h in range(2):
                            f_evac(bks[h], B_Xs[s2b * 2 + h], Xs[HS[h], s2b * 4:(s2b + 1) * 4, :, :].rearrange("p a r e -> p (a r e)"),
                                   psb[bks[h]][HS[h], :])

                def f_s1(q):
                    Xs, B_Xs, Ps, B_Ps = XsL[q], B_XsL[q], PsL[q], B_PsL[q]
                    for eb in range(16):
                        bks = f_banks()
                        for i in range(4):
                            e_ = eb * 4 + i
                            for ri in range(2):
                                for h in range(2):
                                    I("pe", "matmul", [B_Xs, B_fc], B_ps[bks[h]], psb[bks[h]][HS[h], i * 128:(i + 1) * 128],
                                      Xs[HS[h], :, ri, e_], F01[HS[h], ri, :], start=(ri == 0), stop=(ri == 1))
                        for h in range(2):
                            f_evac(bks[h], B_Ps[eb * 2 + h], Ps[HS[h], :, :, eb * 4:(eb + 1) * 4].rearrange("p r k e -> p (r k) e"),
                                   psb[bks[h]][HS[h], :].rearrange("p (e r) -> p r e", e=4))

                def f_s3(q):
                    Ps, B_Ps = PsL[q], B_PsL[q]
                    for kb in range(8):
                        bks = f_banks()
                        for i in range(8):
                            k1 = kb * 8 + i
                            for ri in range(2):
                                for h in range(2):
                                    I("pe", "matmul", [B_Ps, B_fc], B_ps[bks[h]], psb[bks[h]][HS[h], i * 64:(i + 1) * 64],
                                      Ps[HS[h], ri, k1, :], Mc[HS[h], k1, ri, :], start=(ri == 0), stop=(ri == 1))
                        for h in range(2):
                            f_evac(bks[h], B_yf[q * 16 + kb * 2 + h], yf[q][HS[h], :].rearrange("p (k2 k1) -> p k1 k2", k1=64)[:, kb * 8:(kb + 1) * 8, :],
                                   psb[bks[h]][HS[h], :].rearrange("p (a b) -> p a b", a=8))
                    DMA("sp", yTd[s, q], yf[q][:], B_yf[q * 16:(q + 1) * 16], B_y[s])

                f_s0(0)
                f_s0(1)
                f_s1(0)
                f_s1(1)
                f_s3(0)
                f_s3(1)
                P.flush()

            with ExitStack() as es:
                W = [sb(es, f"l_W{i}", [128, S], F32) for i in range(7)]
                ubf = sb(es, "l_ubf", [128, S], BF16)
                gl = sb(es, "l_gl", [128, S], BF16)
                yl = sb(es, "l_yl", [128, S], BF16)
                B_W = [P.buf(f"l_W{i}", 8) for i in range(7)]
                B_ubf = P.buf("l_ubf")
                B_gl = P.buf("l_gl")
                B_yl = P.buf("l_yl")
                lst = {"pbank": 0}
                roles = [list(range(7))]
                for c in range(1, 3):
                    r = roles[-1]
                    roles.append([r[2], r[3], r[1], r[4], r[5], r[6], r[0]])

                def RW(c, k):
                    return W[roles[c][k]], B_W[roles[c][k]]

                def l_front(c):
                    (px, Bpx), (u, Bu) = RW(c, 0), RW(c, 1)
                    DMAs("sp", [(px[:, 0:2048], ppx[s, c, :, 0:2048]), (px[:, 2048:4096], ppx[s, c, :, 2048:4096])], B_p[s], [Bpx])
                    cw = lambda k: convw[:, l, c, k:k + 1]
                    I("act", "activation", [Bpx, Bs], [Bu], out=u[:], in_=px[:], func=AF.Identity, scale=cw(2), bias=convb[:, l, c:c + 1])
                    I("dve", "scalar_tensor_tensor", [Bpx, Bu, Bs], [Bu], out=u[:, 2:S], in0=px[:, 0:S - 2], scalar=cw(0), in1=u[:, 2:S], op0=ALU.mult, op1=ALU.add)
                    I("dve", "scalar_tensor_tensor", [Bpx, Bu, Bs], [Bu], out=u[:, 1:S], in0=px[:, 0:S - 1], scalar=cw(1), in1=u[:, 1:S], op0=ALU.mult, op1=ALU.add)
                    I("dve", "scalar_tensor_tensor", [Bpx, Bu, Bs], [Bu], out=u[:, 0:S - 1], in0=px[:, 1:S], scalar=cw(3), in1=u[:, 0:S - 1], op0=ALU.mult, op1=ALU.add)

                def l_cast(c):
                    (u, Bu) = RW(c, 1)
                    I("act", "activation", [Bu], [B_ubf], out=ubf[:], in_=u[:], func=AF.Identity)

                def l_sig(c, dr_):
                    (ra, Bra), (ib, Bib) = RW(c, 2 + 2 * dr_), RW(c, 3 + 2 * dr_)
                    (u, Bu) = RW(c, 1)
                    vi = (l * 2 + dr_) * 3 + c
                    for tt in range(NTT):
                        tsl = slice(tt * TT, (tt + 1) * TT)
                        for ax, (dstw, Bd) in enumerate(((ra, Bra), (ib, Bib))):
                            bk = lst["pbank"]
                            lst["pbank"] = (bk + 1) % 8
                            I("pe", "matmul", [B_ubf, B_wg], B_ps[bk], psb[bk], WG[:, dr_, ax, c, :], ubf[:, tsl], start=True, stop=True)
                            I("act", "activation", B_ps[bk] + [Bs], Bd[tt], out=dstw[:, tsl], in_=psb[bk], func=AF.Sigmoid,
                              bias=lruv[:, ax, vi:vi + 1])
                    I("dve", "tensor_tensor", [Bib, Bu], [Bib], out=ib[:], in0=ib[:], in1=u[:], op=ALU.mult)

                def l_rest(c, dr_):
                    (ra, Bra), (ib, Bib) = RW(c, 2 + 2 * dr_), RW(c, 3 + 2 * dr_)
                    (tmpb, Btmp) = RW(c, 0 if dr_ == 0 else 6)
                    vi = (l * 2 + dr_) * 3 + c
                    I("act", "activation", [Bra, B_const], [Bra], out=ra[:], in_=ra[:], func=AF.Exp, scale=clam[:, vi:vi + 1])
                    I("act", "activation", [Bra], [Btmp], out=tmpb[:], in_=ra[:], func=AF.Square)
                    I("act", "activation", [Btmp], [Btmp], out=tmpb[:], in_=tmpb[:], func=AF.Sqrt, scale=-1.0, bias=1.0)
                    I("pool", "tensor_tensor", [Bib, Btmp], [Bib], out=ib[:], in0=ib[:], in1=tmpb[:], op=ALU.mult)
                    if dr_ == 0:
                        I("dve", "tensor_tensor_scan", [Bra, Bib], [Btmp], out=tmpb[:], data0=ra[:], data1=ib[:], initial=0.0,
                          op0=ALU.mult, op1=ALU.add)
                    else:
                        I("dve", "tensor_tensor_scan", [Bra, Bib], [Btmp], out=tmpb[:, ::-1], data0=ra[:, ::-1], data1=ib[:, ::-1],
                          initial=0.0, op0=ALU.mult, op1=ALU.add)

                def l_final(c):
                    (h0, Bh0), (h1, Bh1) = RW(c, 0), RW(c, 6)
                    DMA("sp", gl[:], pgl[s, c], B_p[s], [B_gl])
                    I("dve", "tensor_tensor", [Bh0, Bh1], [Bh0], out=h0[:], in0=h0[:], in1=h1[:], op=ALU.add)
                    I("dve", "tensor_tensor", [Bh0, B_gl], [B_yl], out=yl[:], in0=h0[:], in1=gl[:], op=ALU.mult)
                    DMA("sp", yTd[s, 2 + c], yl[:], [B_yl], B_y[s])

                l_front(0)
                l_cast(0)
                for c in range(3):
                    l_sig(c, 0)
                    l_rest(c, 0)
                    l_sig(c, 1)
                    if c + 1 < 3:
                        l_front(c + 1)
                    l_rest(c, 1)
                    if c + 1 < 3:
                        l_cast(c + 1)
                    l_final(c)
                P.flush()

            es_nc = ExitStack()
            wo = sb(es_nc, "c_wo", [128, 8, D], BF16)
            with ExitStack() as es:
                qTd = [sb(es, f"n_q{i}", [128, S], BF16) for i in range(2)]
                kzd = [[sb(es, f"n_kz{i}_{h}", [128, S], BF16) for h in range(2)] for i in range(2)]
                vpd = [sb(es, f"n_vp{i}", [128, 32, 128], BF16) for i in range(2)]
                ynd = [sb(es, f"n_yn{i}", [128, S], BF16) for i in range(2)]
                pe_ = [sb(es, f"n_pe{i}", [128, 2, 5, 128], BF16) for i in range(2)]
                pt_ = [sb(es, f"n_pt{i}", [128, 2, 5, 128], BF16) for i in range(3)]
                lnd = [sb(es, f"n_lnd{i}", [128, 128], F32) for i in range(2)]
                B_qd = P.buf("n_q", 2)
                B_kd = P.buf("n_k", 2)
                B_vd = P.buf("n_v", 2)
                B_ynd = P.buf("n_yn", 64)
                B_pe = P.buf("n_pe", 6)
                B_pt = P.buf("n_pt", 3)
                B_lnd = P.buf("n_lnd", 2)
                for i in range(2):
                    for h in range(2):
                        I("pool", "memset", [], B_kd[i], kzd[i][h][:], 0.0)
                nb = 0
                B_wo_pre = P.buf("n_wo_pre", 8)
                for k in range(8):
                    DMAs("pool", [(wo[:, k, 0:512], w_out[l, k * 128:(k + 1) * 128, 0:512]),
                                  (wo[:, k, 512:1024], w_out[l, k * 128:(k + 1) * 128, 512:1024])], [], B_wo_pre[k])

                def n_loads(hp_):
                    d_ = hp_ % 2
                    DMA("sp", qTd[d_][:], pq[s, hp_], B_p[s], B_qd[d_])
                    DMAs("sp", [(kzd[d_][0][0:64, :], pk[s, hp_, 0:64, :]), (kzd[d_][1][64:128, :], pk[s, hp_, 64:128, :])],
                         B_p[s] + B_kd[d_], B_kd[d_])
                    DMAs("sp", [(vpd[d_][:, half * 16:(half + 1) * 16, :],
                                 pv[s, half * 16:(half + 1) * 16, :, hp_ * 128:(hp_ + 1) * 128].rearrange("t p c -> p t c"))
                                for half in range(2)], B_p[s], B_vd[d_])

                n_loads(0)
                for hp in range(3):
                    d = hp % 2
                    qT, kz, vp, yn = qTd[d], kzd[d], vpd[d], ynd[d]
                    B_q, B_k, B_v = B_qd[d], B_kd[d], B_vd[d]
                    if hp + 1 < 3:
                        n_loads(hp + 1)

                    def blk_info(m):
                        if m < 2:
                            return list(range(4)), EB[:, m, 2 * hp:2 * hp + 2].rearrange("p h j b q -> p h (j b q)")
                        if m >= 30:
                            return list(range(28, 32)), EB[:, m - 28, 2 * hp:2 * hp + 2].rearrange("p h j b q -> p h (j b q)")
                        return list(range(m - 2, m + 3)), EI[:, 2 * hp:2 * hp + 2].rearrange("p h j b q -> p h (j b q)")

                    def scores(m, par):
                        js, tab = blk_info(m)
                        nj = len(js)
                        b0 = 3 * par
                        qs = slice(m * 128, (m + 1) * 128)
                        for hh in range(2):
                            hsl = slice(hh * 64, (hh + 1) * 64)
                            for jj, j in enumerate(js[:4]):
                                I("pe", "matmul", B_q + B_k, B_ps[b0 + hh], psb[b0 + hh][:, jj * 128:(jj + 1) * 128],
                                  kz[hh][:, j * 128:(j + 1) * 128], qT[:, qs], start=True, stop=True)
                            if nj == 5:
                                j = js[4]
                                I("pe", "matmul", B_q + B_k, B_ps[b0 + 2], psb[b0 + 2][:, hh * 128:(hh + 1) * 128],
                                  kz[hh][:, j * 128:(j + 1) * 128], qT[:, qs], start=True, stop=True)
                        for hh in range(2):
                            I("act", "activation", B_ps[b0 + hh], B_pe[par * 3 + hh], out=pe_[par][:, hh, 0:4, :].rearrange("p j q -> p (j q)"),
                              in_=psb[b0 + hh][:], func=AF.Exp, scale=0.125)
                        if nj == 5:
                            I("act", "activation", B_ps[b0 + 2], B_pe[par * 3 + 2], out=pe_[par][:, :, 4, :],
                              in_=psb[b0 + 2][:, 0:256].rearrange("p (h q) -> p h q", h=2), func=AF.Exp, scale=0.125)
                        nq = nj * 128
                        I("dve", "tensor_tensor", B_pe[par * 3:(par + 1) * 3] + [B_tab], B_pt[m % 3],
                          out=pt_[m % 3][:].rearrange("p h j q -> p h (j q)")[:, :, 0:nq],
                          in0=pe_[par][:].rearrange("p h j q -> p h (j q)")[:, :, 0:nq], in1=tab, op=ALU.mult)

                    def pv_(m, par):
                        js, tab = blk_info(m)
                        nj = len(js)
                        tot = 2 * nj
                        n = 0
                        for hh in range(2):
                            vv = vp
                            oo = onesA if hh == 0 else onesB
                            for jj, j in enumerate(js):
                                hsl = slice(hh * 64, (hh + 1) * 64)
                                I("pe", "matmul", B_pt[m % 3] + B_v, B_ps[6], psb[6][hsl, 0:128], vv[:, j, hsl], pt_[m % 3][:, hh, jj, :],
                                  start=(jj == 0), stop=(jj == nj - 1))
                                I("pe", "matmul", B_pt[m % 3] + [B_const], B_ps[7], psb[7][hsl, 0:128], ones[:, 0:64], pt_[m % 3][:, hh, jj, :],
                                  start=(jj == 0), stop=(jj == nj - 1))
                                n += 1
                        I("act", "activation", B_ps[7], B_lnd[par], out=lnd[par][:], in_=psb[7][:, 0:128], func=AF.Ln)
                        I("act", "activation", B_lnd[par], B_lnd[par], out=lnd[par][:], in_=lnd[par][:], func=AF.Exp, scale=-1.0)
                        I("dve", "tensor_tensor", B_ps[6] + B_lnd[par], B_ynd[d * 32 + m], out=yn[:, m * 128:(m + 1) * 128], in0=psb[6][:, 0:128],
                          in1=lnd[par][:], op=ALU.mult)

                    scores(0, 0)
                    scores(1, 1)
                    for m in range(32):
                        if m + 2 < 32:
                            scores(m + 2, m % 2)
                        pv_(m, m % 2)
                        nb += 1
                    DMA("sp", yTd[s, 5 + hp], yn[:], B_ynd[d * 32:(d + 1) * 32], B_y[s])
                P.flush()

            with ExitStack() as es:
                yt = [sb(es, f"c_yt{i}", [128, 8, TT], BF16) for i in range(2)]
                xt = [sb(es, f"c_xt{i}", [128, 8, TT], F32) for i in range(3)]
                B_wo = P.buf("c_wo", 8)
                B_yt = P.buf("c_yt", 2)
                B_xt = P.buf("c_xt", 24)
                sq = sb(es, "c_sq", [128, 8, TT], BF16)
                rs = [sb(es, f"c_rs{i}", [128, TT], F32) for i in range(2)]
                tmp = [sb(es, f"c_tmp{i}", [128, TT], F32) for i in range(2)]
                h2 = [sb(es, f"c_h2{i}", [128, 8, TT], BF16) for i in range(2)]
                B_sq = P.buf("c_sq")
                B_rs = P.buf("c_rs", 2)
                B_tmp = P.buf("c_tmp", 2)
                B_h2 = P.buf("c_h2", 16)
                ntmp = 0
                pbank = 0
                def c_load(tt):
                    pp = tt % 2
                    p3 = tt % 3
                    tsl = slice(tt * TT, (tt + 1) * TT)
                    DMA("sp", yt[pp][:], yTd[s, :, :, tsl].rearrange("c p t -> p c t"), B_y[s], B_yt[pp])
                    DMAs("sp", [(xt[p3][:, 0:4, :], x_src[s, 0:4, :, tsl].rearrange("c p t -> p c t")),
                                (xt[p3][:, 4:8, :], x_src[s, 4:8, :, tsl].rearrange("c p t -> p c t"))],
                         B_xs[s * NTT + tt], B_xt[p3 * 8:(p3 + 1) * 8])

                def c_norm_steps(tt):
                    pp = tt % 2
                    p3 = tt % 3
                    tsl = slice(tt * TT, (tt + 1) * TT)
                    Bx = B_xt[p3 * 8:(p3 + 1) * 8]
                    steps = []
                    steps.append(lambda: I("act", "activation", Bx, [B_sq], out=sq[:], in_=xt[p3][:], func=AF.Square))

                    def ssmm():
                        for c in range(8):
                            I("pe", "matmul", [B_sq, B_const], B_ps[7], psb[7], ones[:], sq[:, c, :], start=(c == 0), stop=(c == 7))
                    steps.append(ssmm)

                    def lnexp():
                        I("act", "activation", B_ps[7] + [B_const], B_rs[pp], out=rs[pp][:], in_=psb[7], func=AF.Ln, scale=1.0 / D, bias=epsb[:])
                        I("act", "activation", B_rs[pp], B_rs[pp], out=rs[pp][:], in_=rs[pp][:], func=AF.Exp, scale=-0.5)
                    steps.append(lnexp)
                    for c in range(8):
                        def hstep(c=c):
                            q = ntmp_[0] % 2
                            ntmp_[0] += 1
                            I("dve", "tensor_tensor", B_xt[p3 * 8 + c] + B_rs[pp], B_tmp[q], out=tmp[q][:], in0=xt[p3][:, c, :], in1=rs[pp][:], op=ALU.mult)
                            I("act", "activation", B_tmp[q] + [B_mod], B_h2[pp * 8 + c], out=h2[pp][:, c, :], in_=tmp[q][:],
                              func=AF.Identity, scale=A2[:, c, s:s + 1], bias=mod[:, 24 + c, s:s + 1])
                        steps.append(hstep)
                    steps.append(lambda: DMA("sp", h2d[s, :, :, tsl].rearrange("c p t -> p c t"), h2[pp][:], B_h2[pp * 8:(pp + 1) * 8], B_h2d[s * NTT + tt]))
                    return steps

                ntmp_ = [0]
                c_sched = {0: [0], 2: [1], 3: [2, 3, 4], 4: [5, 6], 5: [7, 8], 6: [9, 10], 7: [11]}
                c_load(0)
                for tt in range(NTT):
                    pp = tt % 2
                    p3 = tt % 3
                    tsl = slice(tt * TT, (tt + 1) * TT)
                    if tt + 1 < NTT:
                        c_load(tt + 1)
                    nsteps = c_norm_steps(tt - 1) if tt >= 1 else []
                    for m in range(8):
                        if nsteps:
                            for si in c_sched.get(m, []):
                                nsteps[si]()
                        bk = pbank
                        pbank = (pbank + 1) % 7
                        for k in range(8):
                            I("pe", "matmul", B_yt[pp] + [B_wo], B_ps[bk], psb[bk], wo[:, k, m * 128:(m + 1) * 128], yt[pp][:, k, :],
                              start=(k == 0), stop=(k == 7))
                        I("dve", "scalar_tensor_tensor", B_ps[bk] + B_xt[p3 * 8 + m] + [B_mod], B_xt[p3 * 8 + m], out=xt[p3][:, m, :],
                          in0=psb[bk], scalar=mod[:, 16 + m, s:s + 1], in1=xt[p3][:, m, :], op0=ALU.mult, op1=ALU.add)
                    DMAs("sp", [(xs[s, 0:4, :, tsl].rearrange("c p t -> p c t"), xt[p3][:, 0:4, :]),
                                (xs[s, 4:8, :, tsl].rearrange("c p t -> p c t"), xt[p3][:, 4:8, :])],
                         B_xt[p3 * 8:(p3 + 1) * 8], B_xs[s * NTT + tt])
                for st in c_norm_steps(NTT - 1):
                    st()
                P.flush()
            es_nc.close()

        es_layer.close()
        with ExitStack() as es:
            wg = sb(es, "d_wg", [128, 8, D_FF], BF16)
            wu = sb(es, "d_wu", [128, 8, D_FF], BF16)
            wd = sb(es, "d_wd", [128, NFF, D], BF16)
            xt = sb(es, "d_xt", [128, 8, TT], F32)
            h2 = [sb(es, f"d_h2{i}", [128, 8, TT], BF16) for i in range(2)]
            act = sb(es, "d_act", [128, NFF, TT], BF16)
            sg = [sb(es, f"d_sg{i}", [128, TT], BF16) for i in range(2)]
            last = (l == NL - 1)
            if last:
                fsq = sb(es, "d_fsq", [128, 8, TT], BF16)
                frs = sb(es, "d_frs", [128, TT], F32)
                B_fsq = P.buf("d_fsq")
                B_frs = P.buf("d_frs")
            B_wgu = P.buf("d_wgu", NFF)
            B_wd = P.buf("d_wd", NFF)
            B_xt = P.buf("d_xt", 8)
            B_h2 = P.buf("d_h2", 2)
            B_act = P.buf("d_act", NFF)
            B_sg = P.buf("d_sg", 2)
            for j in range(NFF):
                cs_ = slice(j * 128, (j + 1) * 128)
                DMAs("pool", [(wg[:, :, cs_], w_gate[l, :, cs_].rearrange("(k p) c -> p k c", p=128)),
                              (wu[:, :, cs_], w_up[l, :, cs_].rearrange("(k p) c -> p k c", p=128))], [], B_wgu[j])
            for j in range(NFF):
                DMAs("pool", [(wd[:, j, 0:512], w_down[l, j * 128:(j + 1) * 128, 0:512]),
                              (wd[:, j, 512:1024], w_down[l, j * 128:(j + 1) * 128, 512:1024])], [], B_wd[j])
            nsg = 0

            def d_load(n):
                s_, tt_ = divmod(n, NTT)
                DMA("sp", h2[n % 2][:], h2d[s_, :, :, tt_ * TT:(tt_ + 1) * TT].rearrange("c p t -> p c t"),
                    B_h2d[s_ * NTT + tt_], B_h2[n % 2])

            d_load(0)
            n = 0
            pending = []

            def x_loads(s_, tt_):
                tsl_ = slice(tt_ * TT, (tt_ + 1) * TT)
                DMAs("sp", [(xt[:, c, :], xs[s_, c, :, tsl_]) for c in range(4)], B_xs[s_ * NTT + tt_], B_xt[0:4])
                DMAs("sp", [(xt[:, c, :], xs[s_, c, :, tsl_]) for c in range(4, 8)], B_xs[s_ * NTT + tt_], B_xt[4:8])

            def fin_steps(s_, tt_):
                tsl_ = slice(tt_ * TT, (tt_ + 1) * TT)
                st = []
                st.append(lambda: I("act", "activation", [B_xt], [B_fsq], out=fsq[:], in_=xt[:], func=AF.Square))

                def ssmm():
                    for c in range(8):
                        I("pe", "matmul", [B_fsq, B_const], B_ps[4], psb[4], ones[:], fsq[:, c, :], start=(c == 0), stop=(c == 7))
                st.append(ssmm)

                def lnexp():
                    I("act", "activation", B_ps[4] + [B_const], [B_frs], out=frs[:], in_=psb[4], func=AF.Ln, scale=1.0 / D, bias=epsb[:])
                    I("act", "activation", [B_frs], [B_frs], out=frs[:], in_=frs[:], func=AF.Exp, scale=-0.5)
                st.append(lnexp)

                def outs():
                    for c in range(8):
                        I("dve", "scalar_tensor_tensor", B_xt[c] + [B_frs, Bs], B_xt[c], out=xt[:, c, :], in0=xt[:, c, :],
                          scalar=gfin[:, c:c + 1], in1=frs[:], op0=ALU.mult, op1=ALU.mult)
                        DMA("sp", outT[s_, c, :, tsl_], xt[:, c, :], B_xt[c], [B_out])
                st.append(outs)
                return st

            for s in range(NSEQ):
                for tt in range(NTT):
                    tsl = slice(tt * TT, (tt + 1) * TT)
                    hp_ = n % 2
                    n += 1
                    if n < NSEQ * NTT:
                        d_load(n)
                    if not pending:
                        x_loads(s, tt)
                    f_sched = {0: 0, 2: 1, 3: 2, 4: 3}
                    for j in range(NFF):
                        if pending and j in f_sched:
                            pending[f_sched[j]]()
                            if j == 4:
                                pending = []
                                x_loads(s, tt)
                        bg = j % 2
                        bu = 2 + j % 2
                        for k in range(8):
                            I("pe", "matmul", B_h2[hp_] + B_wgu[j], B_ps[bg], psb[bg][:], wg[:, k, j * 128:(j + 1) * 128], h2[hp_][:, k, :],
                              start=(k == 0), stop=(k == 7))
                        for k in range(8):
                            I("pe", "matmul", B_h2[hp_] + B_wgu[j], B_ps[bu], psb[bu][:], wu[:, k, j * 128:(j + 1) * 128], h2[hp_][:, k, :],
                              start=(k == 0), stop=(k == 7))
                        q = nsg % 2
                        nsg += 1
                        I("act", "activation", B_ps[bg], B_sg[q], out=sg[q][:], in_=psb[bg][:], func=AF.Silu)
                        I("dve", "tensor_tensor", B_ps[bu] + B_sg[q], B_act[j], out=act[:, j, :], in0=psb[bu][:], in1=sg[q][:], op=ALU.mult)
                    for m in range(8):
                        bk = 4 + m % 4
                        for j in range(NFF):
                            I("pe", "matmul", B_act[j] + B_wd[j], B_ps[bk], psb[bk][:], wd[:, j, m * 128:(m + 1) * 128], act[:, j, :],
                              start=(j == 0), stop=(j == NFF - 1))
                        I("dve", "scalar_tensor_tensor", B_ps[bk] + B_xt[m] + [B_mod], B_xt[m], out=xt[:, m, :],
                          in0=psb[bk][:], scalar=mod[:, 40 + m, s:s + 1], in1=xt[:, m, :], op0=ALU.mult, op1=ALU.add)
                        if not last:
                            DMA("sp", xs[s, m, :, tsl], xt[:, m, :], B_xt[m], B_xs[s * NTT + tt])
                    if last:
                        pending = fin_steps(s, tt)
            for st_ in pending:
                st_()
            P.flush()

    P.op("sp", None, reads=[B_out])
    P.flush()
    es_glob.close()
    build.stats = dict(P.total)
    return nc


def _prep_shared(inp, NL):
    f = lambda a: np.ascontiguousarray(np.asarray(a, dtype=np.float32))
    cd, Fm, Mc, cm = _consts()
    sh = {}
    sh["w_ada"] = f(inp["w_ada"][:NL])
    sh["b_adaT"] = f(np.asarray(inp["b_ada"])[:NL].reshape(NL, 48, 128).transpose(0, 2, 1))
    g = np.stack([np.asarray(inp["g_mix"])[:NL], np.asarray(inp["g_ffn"])[:NL]], axis=1)
    sh["gT"] = f(g.reshape(NL, 2, 8, 128).transpose(3, 0, 1, 2))
    sh["gfin"] = f(np.asarray(inp["g_final"]).reshape(8, 128).T)
    sh["w_in"] = f(inp["w_in"][:NL])
    sh["w_out"] = f(inp["w_out"][:NL])
    sh["w_gate"] = f(inp["w_ffn_gate"][:NL])
    sh["w_up"] = f(inp["w_ffn_up"][:NL])
    sh["w_down"] = f(inp["w_ffn_down"][:NL])
    sh["w_four"] = f(inp["w_fourier"][:NL])
    sh["convw"] = f(np.asarray(inp["conv_w"])[:NL].reshape(NL, 4, 3, 128).transpose(3, 0, 2, 1))
    sh["convb"] = f(np.asarray(inp["conv_b"])[:NL].reshape(NL, 3, 128).transpose(2, 0, 1))
    sh["lru_wa"] = f(inp["lru_w_a"][:NL])
    sh["lru_wx"] = f(inp["lru_w_x"][:NL])
    v = np.stack([np.asarray(inp["lru_b_a"])[:NL], np.asarray(inp["lru_b_x"])[:NL],
                  np.asarray(inp["lru_lambda"])[:NL]], axis=0)
    sh["lruv"] = f(v.reshape(3, NL, 2, 3, 128).transpose(4, 0, 1, 2, 3))
    kc = np.arange(64)[:, None]
    qc = np.arange(64)[None, :]
    idx = kc - qc + 15
    valid = (idx >= 0) & (idx <= 30)
    rp = np.asarray(inp["na_rpb"], np.float32)[:NL][..., np.clip(idx, 0, 30)]
    rp = np.where(valid, rp, np.float32(0.0))
    sh["rpbtab"] = f(rp.reshape(NL, 90, 64, 64))
    sh["c_cd"] = f(cd)
    sh["c_F"] = f(Fm)
    sh["c_Mc"] = f(Mc)
    sh["c_cm"] = f(cm)
    return sh


def _prep_core(x, c, NSEQ):
    xT = np.ascontiguousarray(np.asarray(x, np.float32).transpose(0, 2, 1).reshape(NSEQ, 8, 128, S))
    cT = np.zeros((128, 8, 2), np.float32)
    cT[:, :, :NSEQ] = np.asarray(c, np.float32).reshape(NSEQ, 8, 128).transpose(2, 1, 0)
    return {"xT": xT, "cT": cT}


def run(inp, NSEQ, NL, n_cores, stop=None, maxops=None):
    nc = build(NSEQ, NL, stop, maxops)
    sh = _prep_shared(inp, NL)
    x = np.asarray(inp["x"])
    c = np.asarray(inp["c"])
    in_maps = []
    for i in range(n_cores):
        m = dict(sh)
        m.update(_prep_core(x[i * NSEQ:(i + 1) * NSEQ], c[i * NSEQ:(i + 1) * NSEQ], NSEQ))
        in_maps.append(m)
    res = run_bass_kernel_spmd(nc, in_maps, core_ids=list(range(n_cores)))
    outs = []
    for i in range(n_cores):
        o = np.asarray(res.results[i]["outT"]).reshape(NSEQ, D, S).transpose(0, 2, 1)
        outs.append(o)
    return np.ascontiguousarray(np.concatenate(outs, axis=0).astype(np.float32))


def kernel(**inputs):
    return run(inputs, 2, DEPTH, N_CORES)
```
